# Optimizing a Trainium2 kernel written in Bass

```python
import math
import jax, jax.numpy as jnp
from jax import lax
import numpy as np

D_MODEL = 1024
BATCH = 8
SEQ = 4096
DEPTH = 2

N_A_LAYERS = DEPTH // 2
N_B_LAYERS = DEPTH - N_A_LAYERS
D_SSM = D_MODEL
SSM_GROUP = 16
N_GROUPS = D_SSM // SSM_GROUP
STATE = 64
N_HEADS = 16
HEAD_DIM = 64
D_ATT = N_HEADS * HEAD_DIM
D_FF = 2816
CONV_W = 3
Q_BLOCK = 128
EPS = 1e-6
DT_MIN = 1e-3
DT_MAX = 1e-1

kernel_name = "yoco_s5_stickbreaking_convffn"


def rms_norm(x, g):
    xf = x.astype(jnp.float32)
    y = xf * lax.rsqrt(jnp.mean(xf * xf, axis=-1, keepdims=True) + EPS)
    return (y * g.astype(jnp.float32)).astype(x.dtype)


def s5_mixer(h, w_in, a_re, a_im, log_dt, b_re, b_im, c_re, c_im, d_skip, w_glu):
    bsz, seq, _ = h.shape
    f32 = jnp.float32
    u = (h @ w_in).astype(f32).reshape(bsz, seq, N_GROUPS, SSM_GROUP)
    a_re = a_re.astype(f32)
    a_im = a_im.astype(f32)
    dt = jnp.exp(log_dt.astype(f32))[:, None]
    mag = jnp.exp(a_re * dt)
    ab_re = mag * jnp.cos(a_im * dt)
    ab_im = mag * jnp.sin(a_im * dt)
    den = a_re * a_re + a_im * a_im
    f_re = ((ab_re - 1.0) * a_re + ab_im * a_im) / den
    f_im = (ab_im * a_re - (ab_re - 1.0) * a_im) / den
    b_re = b_re.astype(f32)
    b_im = b_im.astype(f32)
    bb_re = f_re[..., None] * b_re - f_im[..., None] * b_im
    bb_im = f_re[..., None] * b_im + f_im[..., None] * b_re
    bu_re = jnp.einsum('blgh,gph->blgp', u, bb_re)
    bu_im = jnp.einsum('blgh,gph->blgp', u, bb_im)
    shape_a = (1, seq, N_GROUPS, STATE)
    a_seq_re = jnp.broadcast_to(ab_re[None, None], shape_a)
    a_seq_im = jnp.broadcast_to(ab_im[None, None], shape_a)

    def combine(e1, e2):
        a1r, a1i, b1r, b1i = e1
        a2r, a2i, b2r, b2i = e2
        return (a2r * a1r - a2i * a1i,
                a2r * a1i + a2i * a1r,
                a2r * b1r - a2i * b1i + b2r,
                a2r * b1i + a2i * b1r + b2i)

    _, _, s_re, s_im = lax.associative_scan(
        combine, (a_seq_re, a_seq_im, bu_re, bu_im), axis=1)
    y = (jnp.einsum('blgp,ghp->blgh', s_re, c_re.astype(f32))
         - jnp.einsum('blgp,ghp->blgh', s_im, c_im.astype(f32))
         + d_skip.astype(f32) * u)
    y = jax.nn.gelu(y.reshape(bsz, seq, D_SSM)).astype(h.dtype)
    z = y @ w_glu
    return z[..., :D_MODEL] * jax.nn.sigmoid(z[..., D_MODEL:])


def stick_breaking_attention(h, w_q, k, v, w_o):
    bsz, seq, _ = h.shape
    scale = HEAD_DIM ** -0.5
    q = (h @ w_q).reshape(bsz, seq, N_HEADS, HEAD_DIM).transpose(0, 2, 1, 3) * scale
    outs = []
    for blk in range(seq // Q_BLOCK):
        t0 = blk * Q_BLOCK
        nk = t0 + Q_BLOCK
        qb = q[:, :, t0:nk]
        kb = k[:, :, :nk]
        vb = v[:, :, :nk]
        z = jnp.einsum('bhqd,bhkd->bhqk', qb, kb).astype(jnp.float32)
        t_idx = t0 + jnp.arange(Q_BLOCK)[:, None]
        s_idx = jnp.arange(nk)[None, :]
        causal = s_idx < t_idx
        log_beta = jax.nn.log_sigmoid(z)
        log_one_minus = jnp.where(causal, log_beta - z, 0.0)
        rem = lax.cumsum(log_one_minus, axis=3, reverse=True) - log_one_minus
        w = jnp.where(causal, jnp.exp(log_beta + rem), 0.0)
        outs.append(jnp.einsum('bhqk,bhkd->bhqd', w.astype(vb.dtype), vb))
    o = jnp.concatenate(outs, axis=2).transpose(0, 2, 1, 3).reshape(bsz, seq, D_ATT)
    return o @ w_o


def conv_ffn(h, w_up, conv_w, conv_b, w_down):
    gu = h @ w_up
    g = gu[..., :D_FF]
    u = gu[..., D_FF:]
    g = lax.conv_general_dilated(
        g, conv_w, window_strides=(1,), padding=[(CONV_W - 1, 0)],
        dimension_numbers=('NWC', 'WIO', 'NWC'), feature_group_count=D_FF) + conv_b
    return (jax.nn.silu(g) * u) @ w_down


def setup_inputs(seed: int = 0) -> dict:
    key = jax.random.key(seed)
    ks = jax.random.split(key, 24)
    f32 = jnp.float32
    nrm = lambda k, shape, s: jax.random.normal(k, shape, f32) * s
    n_idx = jnp.arange(STATE, dtype=f32)
    return {
        "x": jax.random.normal(ks[0], (BATCH, SEQ, D_MODEL), f32),
        "norm_mix": 1.0 + nrm(ks[1], (DEPTH, D_MODEL), 0.02),
        "norm_ffn": 1.0 + nrm(ks[2], (DEPTH, D_MODEL), 0.02),
        "norm_kv": 1.0 + nrm(ks[3], (D_MODEL,), 0.02),
        "norm_final": 1.0 + nrm(ks[4], (D_MODEL,), 0.02),
        "ssm_w_in": nrm(ks[5], (N_A_LAYERS, D_MODEL, D_SSM), D_MODEL ** -0.5),
        "ssm_a_re": -0.5 * jnp.exp(nrm(ks[6], (N_A_LAYERS, N_GROUPS, STATE), 0.05)),
        "ssm_a_im": math.pi * n_idx + nrm(ks[7], (N_A_LAYERS, N_GROUPS, STATE), 0.01),
        "ssm_log_dt": jax.random.uniform(ks[8], (N_A_LAYERS, N_GROUPS), f32,
                                         minval=math.log(DT_MIN), maxval=math.log(DT_MAX)),
        "ssm_b_re": nrm(ks[9], (N_A_LAYERS, N_GROUPS, STATE, SSM_GROUP), (2 * SSM_GROUP) ** -0.5),
        "ssm_b_im": nrm(ks[10], (N_A_LAYERS, N_GROUPS, STATE, SSM_GROUP), (2 * SSM_GROUP) ** -0.5),
        "ssm_c_re": nrm(ks[11], (N_A_LAYERS, N_GROUPS, SSM_GROUP, STATE), (2 * STATE) ** -0.5),
        "ssm_c_im": nrm(ks[12], (N_A_LAYERS, N_GROUPS, SSM_GROUP, STATE), (2 * STATE) ** -0.5),
        "ssm_d": nrm(ks[13], (N_A_LAYERS, N_GROUPS, SSM_GROUP), 1.0),
        "ssm_w_glu": nrm(ks[14], (N_A_LAYERS, D_SSM, 2 * D_MODEL), D_SSM ** -0.5),
        "kv_w": nrm(ks[15], (D_MODEL, 2 * D_ATT), D_MODEL ** -0.5),
        "attn_w_q": nrm(ks[16], (N_B_LAYERS, D_MODEL, D_ATT), D_MODEL ** -0.5),
        "attn_w_o": nrm(ks[17], (N_B_LAYERS, D_ATT, D_MODEL), D_ATT ** -0.5),
        "ffn_w_up": nrm(ks[18], (DEPTH, D_MODEL, 2 * D_FF), D_MODEL ** -0.5),
        "ffn_conv_w": nrm(ks[19], (DEPTH, CONV_W, 1, D_FF), CONV_W ** -0.5),
        "ffn_conv_b": nrm(ks[20], (DEPTH, D_FF), 0.01),
        "ffn_w_down": nrm(ks[21], (DEPTH, D_FF, D_MODEL), D_FF ** -0.5),
    }


def reference(x, norm_mix, norm_ffn, norm_kv, norm_final,
              ssm_w_in, ssm_a_re, ssm_a_im, ssm_log_dt, ssm_b_re, ssm_b_im,
              ssm_c_re, ssm_c_im, ssm_d, ssm_w_glu,
              kv_w, attn_w_q, attn_w_o,
              ffn_w_up, ffn_conv_w, ffn_conv_b, ffn_w_down):
    bsz, seq, _ = x.shape
    k_shared = None
    v_shared = None
    for layer in range(DEPTH):
        h = rms_norm(x, norm_mix[layer])
        if layer < N_A_LAYERS:
            x = x + s5_mixer(h, ssm_w_in[layer], ssm_a_re[layer], ssm_a_im[layer],
                             ssm_log_dt[layer], ssm_b_re[layer], ssm_b_im[layer],
                             ssm_c_re[layer], ssm_c_im[layer], ssm_d[layer],
                             ssm_w_glu[layer])
        else:
            j = layer - N_A_LAYERS
            x = x + stick_breaking_attention(h, attn_w_q[j], k_shared, v_shared, attn_w_o[j])
        x = x + conv_ffn(rms_norm(x, norm_ffn[layer]), ffn_w_up[layer],
                         ffn_conv_w[layer], ffn_conv_b[layer], ffn_w_down[layer])
        if layer == N_A_LAYERS - 1:
            kv = rms_norm(x, norm_kv) @ kv_w
            k_shared = kv[..., :D_ATT].reshape(bsz, seq, N_HEADS, HEAD_DIM).transpose(0, 2, 1, 3)
            v_shared = kv[..., D_ATT:].reshape(bsz, seq, N_HEADS, HEAD_DIM).transpose(0, 2, 1, 3)
    return rms_norm(x, norm_final)
```

```python
import contextlib
import math
import numpy as np
import concourse.bass as bass
import concourse.mybir as mybir
from concourse.bass_utils import run_bass_kernel_spmd

F32 = mybir.dt.float32
BF16 = mybir.dt.bfloat16
I32 = mybir.dt.int32
AF = mybir.ActivationFunctionType
ALU = mybir.AluOpType

L = 4096
D = 1024
KC = 8
TS = 512
NT = L // TS
DFF = 2816
FC = DFF // 128
NPAIR = 32
TWO_PI_LO = 6.283185
PI_LO = 3.1415925

ENGS = ("pe", "act", "dve", "pool", "sp")
DMA_RING = 12


class Buf:
    __slots__ = ("name", "w", "r")

    def __init__(self, name=""):
        self.name = name
        self.w = None
        self.r = []


class Op:
    __slots__ = ("eng", "fn", "deps", "dma", "need", "sem", "val", "idx", "ring_prev")

    def __init__(self, eng, fn, dma):
        self.eng = eng
        self.fn = fn
        self.dma = dma
        self.deps = []
        self.need = False
        self.sem = None
        self.val = 0
        self.ring_prev = None


class Glob:
    def __init__(self, nc, es):
        self.nc = nc
        self.esem = {e: es.enter_context(nc.semaphore("c_" + e)) for e in ENGS}
        self.rings = {
            e: [es.enter_context(nc.semaphore("d_%s%d" % (e, i))) for i in range(DMA_RING)]
            for e in ("sp", "act", "pool")
        }
        self.cnt = {e: 0 for e in ENGS}
        self.dcnt = {e: 0 for e in self.rings}
        self.ring_last = {}
        self.seen = {e: {} for e in ENGS}


class Prog:
    def __init__(self, g):
        self.g = g
        self.nc = g.nc
        self.ops = []
        self.touched = {}

    def op(self, eng, fn, reads=(), writes=(), dma=False):
        o = Op(eng, fn, dma)
        o.idx = len(self.ops)
        for b in reads:
            self.touched[id(b)] = b
        for b in writes:
            self.touched[id(b)] = b
        deps = set()
        for b in reads:
            if b.w is not None:
                deps.add(b.w)
        for b in writes:
            if b.w is not None:
                deps.add(b.w)
            for r in b.r:
                deps.add(r)
        deps.discard(o.idx)
        o.deps = sorted(deps)
        for b in reads:
            b.r.append(o.idx)
        for b in writes:
            b.w = o.idx
            b.r = []
        self.ops.append(o)
        return o

    def dma(self, q, out, in_, reads=(), writes=()):
        return self.op(q, lambda e: e.dma_start(out=out, in_=in_), reads, writes, dma=True)

    def emit(self):
        g = self.g
        nc = self.nc
        ops = self.ops
        for o in ops:
            for d in o.deps:
                dop = ops[d]
                if dop.eng == "pe" and o.eng == "pe" and not dop.dma and not o.dma:
                    continue
                dop.need = True
        for o in ops:
            if o.dma:
                k = g.dcnt[o.eng]
                g.dcnt[o.eng] += 1
                slot = k % DMA_RING
                o.sem = g.rings[o.eng][slot]
                o.val = 16 * (k // DMA_RING + 1)
                o.ring_prev = g.ring_last.get((o.eng, slot))
                g.ring_last[(o.eng, slot)] = (o.sem, o.val)
            elif o.need:
                g.cnt[o.eng] += 1
                o.sem = g.esem[o.eng]
                o.val = g.cnt[o.eng]
        per = {e: [o for o in ops if o.eng == e] for e in ENGS}
        ring_final = dict(g.ring_last)

        def body(ename, eng):
            seen = g.seen[ename]

            def wait(sem, val):
                key = id(sem)
                if seen.get(key, 0) >= val:
                    return
                seen[key] = val
                eng.wait_ge(sem, val)

            for o in per[ename]:
                want = {}
                for d in o.deps:
                    dop = ops[d]
                    if dop.eng == "pe" and ename == "pe" and not dop.dma and not o.dma:
                        continue
                    key = id(dop.sem)
                    if key not in want or want[key][1] < dop.val:
                        want[key] = (dop.sem, dop.val)
                if o.dma and o.ring_prev is not None:
                    s, v = o.ring_prev
                    key = id(s)
                    if key not in want or want[key][1] < v:
                        want[key] = (s, v)
                for s, v in want.values():
                    wait(s, v)
                ins = o.fn(eng)
                if o.dma:
                    ins.then_inc(o.sem, 16)
                elif o.need:
                    ins.then_inc(o.sem, 1)
            if ename == "sp":
                for (s, v) in ring_final.values():
                    wait(s, v)

        with nc.Block() as block:

            @block.tensor
            def _(e):
                body("pe", e)

            @block.scalar
            def _(e):
                body("act", e)

            @block.vector
            def _(e):
                body("dve", e)

            @block.gpsimd
            def _(e):
                body("pool", e)

            @block.sync
            def _(e):
                body("sp", e)

        for b in self.touched.values():
            b.w = None
            b.r = []


class Rot:
    def __init__(self, items):
        self.items = items
        self.i = 0

    def next(self):
        it = self.items[self.i % len(self.items)]
        self.i += 1
        return it


def build(stop_after=99, dump=None):
    nc = bass.Bass("TRN2", target_bir_lowering=False)

    def din(name, shape, dt=F32):
        return nc.dram_tensor(name, shape, dt, kind="ExternalInput")

    def dscr(name, shape, dt):
        kind = "ExternalOutput" if (dump is not None and name in dump) else "Internal"
        return nc.dram_tensor(name, shape, dt, kind=kind)

    xin = din("xin", [KC, 128, L])
    gam = din("gam", [128, 6, KC])
    w_in = din("w_in", [D, D])
    w_glu = din("w_glu", [D, 2 * D])
    kv_w = din("kv_w", [D, 2 * D])
    w_q = din("w_q", [D, D])
    w_o = din("w_o", [D, D])
    w_up = din("w_up", [2, D, 2 * DFF])
    w_dn = din("w_dn", [2, DFF, D])
    cw = din("cw", [128, 2, FC, 3])
    cb = din("cb", [128, 2, FC])
    s_are = din("s_are", [128, NPAIR])
    s_aim = din("s_aim", [128, NPAIR])
    s_ldt = din("s_ldt", [128, NPAIR])
    s_bre = din("s_bre", [128, NPAIR, 16])
    s_bim = din("s_bim", [128, NPAIR, 16])
    s_cre = din("s_cre", [128, NPAIR, 16])
    s_cim = din("s_cim", [128, NPAIR, 16])
    s_d = din("s_d", [128, KC])
    outT = nc.dram_tensor("outT", [KC, 128, L], F32, kind="ExternalOutput")

    XT = dscr("XT", [KC, 128, L], F32)
    KT = dscr("KTs", [KC, 128, L], BF16)
    QT = dscr("QTs", [KC, 128, L], BF16)
    VS = dscr("VSs", [L // 128, 128, D], BF16)
    UD = dscr("UD", [KC, 128, L], BF16) if (dump and "UD" in dump) else None

    def xtile(dr, i):
        return dr.ap()[:, :, i * TS:(i + 1) * TS].rearrange("c p t -> p c t")

    with contextlib.ExitStack() as top:
        G = Glob(nc, top)

        uniq = [0]

        def sbt(es, name, shape, dt):
            uniq[0] += 1
            return es.enter_context(nc.sbuf_tensor("S%d_%s" % (uniq[0], name), shape, dt))

        def pst(es, name, shape, dt=F32):
            uniq[0] += 1
            return es.enter_context(nc.psum_tensor("P%d_%s" % (uniq[0], name), shape, dt))

        ones_bf = sbt(top, "ones_bf", [128, 128], BF16)
        gam_sb = sbt(top, "gam_sb", [128, 6, KC], F32)

        def load_w_cast(P, dst_fn, src2d, nrows_chunks, ncols, wbuf):
            for rc in range(nrows_chunks):
                c0 = 0
                while c0 < ncols:
                    c1 = min(ncols, c0 + 2048)
                    P.dma("pool", dst_fn(rc, c0, c1), src2d[rc * 128:(rc + 1) * 128, c0:c1], writes=[wbuf])
                    c0 = c1

        def rms_rstd(P, xt, xb, sq, sqb, ssum, ssb, rstd, rsb):
            P.op("act", lambda e: e.activation(out=sq, in_=xt[:].rearrange("p c t -> p (c t)"), func=AF.Square), [xb], [sqb])

            def mm(e):
                r = None
                for kc in range(KC):
                    r = e.matmul(ssum, lhsT=ones_bf[:], rhs=sq[:, kc * TS:(kc + 1) * TS], start=(kc == 0), stop=(kc == KC - 1))
                return r

            P.op("pe", mm, [sqb], [ssb])
            P.op("act", lambda e: e.activation(out=rstd, in_=ssum, func=AF.Sqrt, scale=1.0 / D, bias=1e-6), [ssb], [rsb])
            P.op("dve", lambda e: e.reciprocal(out=rstd, in_=rstd), [rsb], [rsb])

        def rms_apply(P, xt, xb, rstd, rsb, gi, ht, hb):
            for kc in range(KC):
                P.op("dve", lambda e, kc=kc: e.scalar_tensor_tensor(out=ht[:, kc, :], in0=xt[:, kc, :], scalar=gam_sb[:, gi, kc:kc + 1], in1=rstd, op0=ALU.mult, op1=ALU.mult), [xb, rsb], [hb])

        def init_consts():
            with contextlib.ExitStack() as es:
                P = Prog(G)
                P.op("pool", lambda e: e.memset(ones_bf[:], 1.0), [], [Buf()])
                P.dma("sp", gam_sb[:], gam.ap(), writes=[Buf()])
                P.emit()

        def phase_A1(uT):
            with contextlib.ExitStack() as es:
                P = Prog(G)
                w_sb = sbt(es, "a1_w", [128, KC, D], BF16)
                wb = Buf("w")
                load_w_cast(P, lambda rc, c0, c1: w_sb[:, rc, c0:c1], w_in.ap(), KC, D, wb)
                xts = [sbt(es, "a1_x%d" % i, [128, KC, TS], F32) for i in range(2)]
                xbs = [Buf() for _ in range(2)]
                sq = sbt(es, "a1_sq", [128, KC * TS], BF16)
                sqb = Buf()
                rstd = sbt(es, "a1_rstd", [128, TS], F32)
                rsb = Buf()
                hts = [sbt(es, "a1_h%d" % i, [128, KC, TS], BF16) for i in range(2)]
                hbs = [Buf() for _ in range(2)]
                ssum = pst(es, "a1_ss", [128, TS])
                ssb = Buf()
                banks = [pst(es, "a1_b%d" % i, [128, TS]) for i in range(4)]
                bbs = [Buf() for _ in range(4)]
                ub = Buf("uT")
                for i in range(NT):
                    xt, xb = xts[i % 2], xbs[i % 2]
                    ht, hb = hts[i % 2], hbs[i % 2]
                    P.dma("sp", xt[:], xtile(xin, i), writes=[xb])
                    rms_rstd(P, xt, xb, sq[:], sqb, ssum[:], ssb, rstd[:], rsb)
                    rms_apply(P, xt, xb, rstd[:], rsb, 0, ht, hb)
                    for fo in range(KC):
                        bk, bb = banks[fo % 4], bbs[fo % 4]

                        def mm(e, fo=fo, bk=bk, ht=ht):
                            r = None
                            for kc in range(KC):
                                r = e.matmul(bk[:], lhsT=w_sb[:, kc, fo * 128:(fo + 1) * 128], rhs=ht[:, kc, :], start=(kc == 0), stop=(kc == KC - 1))
                            return r

                        P.op("pe", mm, [wb, hb], [bb])
                        dst = uT[:, fo, i * TS:(i + 1) * TS]
                        if fo % 2 == 0:
                            P.op("act", lambda e, bk=bk, dst=dst: e.activation(out=dst, in_=bk[:], func=AF.Copy), [bb], [ub])
                        else:
                            P.op("dve", lambda e, bk=bk, dst=dst: e.tensor_copy(out=dst, in_=bk[:]), [bb], [ub])
                if UD is not None:
                    P.dma("sp", UD.ap().rearrange("c p t -> p c t"), uT[:], reads=[ub], writes=[Buf()])
                P.emit()

        def phase_A2(uT):
            with contextlib.ExitStack() as es:
                P = Prog(G)
                are = sbt(es, "s_are", [128, NPAIR], F32)
                aim = sbt(es, "s_aim", [128, NPAIR], F32)
                ldt = sbt(es, "s_ldt", [128, NPAIR], F32)
                bre = sbt(es, "s_bre", [128, NPAIR, 16], F32)
                bim = sbt(es, "s_bim", [128, NPAIR, 16], F32)
                cre = sbt(es, "s_cre", [128, NPAIR, 16], F32)
                cim = sbt(es, "s_cim", [128, NPAIR, 16], F32)
                dsk = sbt(es, "s_dsk", [128, KC], F32)
                pb = Buf("params")
                for t, s in ((are, s_are), (aim, s_aim), (ldt, s_ldt), (bre, s_bre), (bim, s_bim), (cre, s_cre), (cim, s_cim), (dsk, s_d)):
                    P.dma("sp", t[:], s.ap(), writes=[pb])
                names = ["dt", "xr", "r", "th", "fr", "sn", "hf", "cs", "abre", "abim", "den", "m1", "t1", "t2", "fre", "fim"]
                T = {n: sbt(es, "s_" + n, [128, NPAIR], F32) for n in names}
                ki = sbt(es, "s_ki", [128, NPAIR], I32)
                tb = Buf("tiny")

                def tiny(eng, fn):
                    P.op(eng, fn, [pb, tb], [tb])

                tiny("act", lambda e: e.activation(out=T["dt"][:], in_=ldt[:], func=AF.Exp))
                tiny("dve", lambda e: e.tensor_tensor(out=T["xr"][:], in0=are[:], in1=T["dt"][:], op=ALU.mult))
                tiny("act", lambda e: e.activation(out=T["r"][:], in_=T["xr"][:], func=AF.Exp))
                tiny("dve", lambda e: e.tensor_tensor(out=T["th"][:], in0=aim[:], in1=T["dt"][:], op=ALU.mult))
                tiny("dve", lambda e: e.tensor_scalar(out=T["fr"][:], in0=T["th"][:], scalar1=1.0 / (2.0 * math.pi), scalar2=None, op0=ALU.mult))
                tiny("dve", lambda e: e.tensor_copy(out=ki[:], in_=T["fr"][:]))
                tiny("dve", lambda e: e.tensor_tensor(out=T["fr"][:], in0=T["fr"][:], in1=ki[:], op=ALU.subtract))
                tiny("act", lambda e: e.activation(out=T["sn"][:], in_=T["fr"][:], func=AF.Sin, scale=TWO_PI_LO))
                tiny("act", lambda e: e.activation(out=T["hf"][:], in_=T["fr"][:], func=AF.Sin, scale=PI_LO))
                tiny("dve", lambda e: e.tensor_tensor(out=T["cs"][:], in0=T["hf"][:], in1=T["hf"][:], op=ALU.mult))
                tiny("dve", lambda e: e.tensor_scalar(out=T["cs"][:], in0=T["cs"][:], scalar1=-2.0, scalar2=1.0, op0=ALU.mult, op1=ALU.add))
                tiny("dve", lambda e: e.tensor_tensor(out=T["abre"][:], in0=T["r"][:], in1=T["cs"][:], op=ALU.mult))
                tiny("dve", lambda e: e.tensor_tensor(out=T["abim"][:], in0=T["r"][:], in1=T["sn"][:], op=ALU.mult))
                tiny("dve", lambda e: e.tensor_tensor(out=T["den"][:], in0=are[:], in1=are[:], op=ALU.mult))
                tiny("dve", lambda e: e.tensor_tensor(out=T["t1"][:], in0=aim[:], in1=aim[:], op=ALU.mult))
                tiny("dve", lambda e: e.tensor_tensor(out=T["den"][:], in0=T["den"][:], in1=T["t1"][:], op=ALU.add))
                tiny("dve", lambda e: e.reciprocal(out=T["den"][:], in_=T["den"][:]))
                tiny("dve", lambda e: e.tensor_scalar(out=T["m1"][:], in0=T["abre"][:], scalar1=-1.0, scalar2=None, op0=ALU.add))
                tiny("dve", lambda e: e.tensor_tensor(out=T["t1"][:], in0=T["m1"][:], in1=are[:], op=ALU.mult))
                tiny("dve", lambda e: e.tensor_tensor(out=T["t2"][:], in0=T["abim"][:], in1=aim[:], op=ALU.mult))
                tiny("dve", lambda e: e.tensor_tensor(out=T["t1"][:], in0=T["t1"][:], in1=T["t2"][:], op=ALU.add))
                tiny("dve", lambda e: e.tensor_tensor(out=T["fre"][:], in0=T["t1"][:], in1=T["den"][:], op=ALU.mult))
                tiny("dve", lambda e: e.tensor_tensor(out=T["t1"][:], in0=T["abim"][:], in1=are[:], op=ALU.mult))
                tiny("dve", lambda e: e.tensor_tensor(out=T["t2"][:], in0=T["m1"][:], in1=aim[:], op=ALU.mult))
                tiny("dve", lambda e: e.tensor_tensor(out=T["t1"][:], in0=T["t1"][:], in1=T["t2"][:], op=ALU.subtract))
                tiny("dve", lambda e: e.tensor_tensor(out=T["fim"][:], in0=T["t1"][:], in1=T["den"][:], op=ALU.mult))

                BbT = sbt(es, "s_BbT", [128, NPAIR, 2, 128], BF16)
                CTr = sbt(es, "s_CTr", [128, NPAIR, 128], BF16)
                CTi = sbt(es, "s_CTi", [128, NPAIR, 128], BF16)
                ident = sbt(es, "s_ident", [128, 128], F32)
                carry = sbt(es, "s_carry", [128, NPAIR, 2], F32)
                iota_i = sbt(es, "s_iotai", [128, TS], I32)
                iota_f = sbt(es, "s_iotaf", [128, TS], F32)
                cbuf = Buf("consts")
                matb = Buf("mats")

                def mk_ident(e):
                    e.memset(ident[:], 1.0)
                    return e.affine_select(out=ident[:], in_=ident[:], pattern=[[-1, 128]], compare_op=ALU.is_equal, fill=0.0, base=0, channel_multiplier=1)

                P.op("pool", mk_ident, [], [cbuf])
                P.op("pool", lambda e: e.iota(iota_i[:], [[1, TS]], base=1, channel_multiplier=0), [], [cbuf])
                P.op("pool", lambda e: e.tensor_copy(out=iota_f[:], in_=iota_i[:]), [cbuf], [cbuf])
                P.op("pool", lambda e: e.memset(carry[:], 0.0), [], [cbuf])
                P.op("pool", lambda e: e.memset(CTr[:], 0.0), [], [matb])
                P.op("pool", lambda e: e.memset(CTi[:], 0.0), [], [matb])

                with contextlib.ExitStack() as es2:
                    padr = sbt(es2, "s_padr", [128, NPAIR, 128], F32)
                    padi = sbt(es2, "s_padi", [128, NPAIR, 128], F32)
                    tmp1 = sbt(es2, "s_tmp1", [128, NPAIR, 16], F32)
                    tmp2 = sbt(es2, "s_tmp2", [128, NPAIR, 16], F32)
                    padb = Buf("pad")
                    P.op("pool", lambda e: e.memset(padr[:], 0.0), [], [padb])
                    P.op("pool", lambda e: e.memset(padi[:], 0.0), [], [padb])

                    def padview(t, gl, dtsz=None):
                        return bass.AP(t, 64 * gl * NPAIR * 128 + 16 * gl, [[NPAIR * 128, 64], [512, 8], [160, 4], [1, 16]])

                    def bview(t, gl):
                        return bass.AP(t, 64 * gl * NPAIR * 16, [[NPAIR * 16, 64], [64, 8], [16, 4], [1, 16]])

                    def fview(t, gl):
                        return bass.AP(t, 64 * gl * NPAIR, [[NPAIR, 64], [4, 8], [1, 4], [0, 16]])

                    for gl in range(2):
                        def bb(e, gl=gl):
                            t1v, t2v = bview(tmp1, gl), bview(tmp2, gl)
                            e.tensor_tensor(out=t1v, in0=bview(bre, gl), in1=fview(T["fre"], gl), op=ALU.mult)
                            e.tensor_tensor(out=t2v, in0=bview(bim, gl), in1=fview(T["fim"], gl), op=ALU.mult)
                            return e.tensor_tensor(out=padview(padr, gl), in0=t1v, in1=t2v, op=ALU.subtract)

                        P.op("pool", bb, [pb, tb, padb], [padb])

                        def bb2(e, gl=gl):
                            t1v, t2v = bview(tmp1, gl), bview(tmp2, gl)
                            e.tensor_tensor(out=t1v, in0=bview(bim, gl), in1=fview(T["fre"], gl), op=ALU.mult)
                            e.tensor_tensor(out=t2v, in0=bview(bre, gl), in1=fview(T["fim"], gl), op=ALU.mult)
                            return e.tensor_tensor(out=padview(padi, gl), in0=t1v, in1=t2v, op=ALU.add)

                        P.op("pool", bb2, [pb, tb, padb], [padb])
                        P.op("dve", lambda e, gl=gl: e.tensor_copy(out=padview(CTr, gl), in_=bview(cre, gl)), [pb, matb], [matb])
                        P.op("dve", lambda e, gl=gl: e.tensor_scalar(out=padview(CTi, gl), in0=bview(cim, gl), scalar1=-1.0, scalar2=None, op0=ALU.mult), [pb, matb], [matb])

                    trb = [pst(es2, "s_trb%d" % i, [128, TS]) for i in range(2)]
                    trbb = [Buf() for _ in range(2)]
                    n = 0
                    for j0 in range(0, NPAIR, 2):
                        bk, bb_ = trb[n % 2], trbb[n % 2]
                        n += 1

                        def trs(e, j0=j0, bk=bk):
                            r = None
                            for jj in range(2):
                                for ri, pad in enumerate((padr, padi)):
                                    k = jj * 2 + ri
                                    r = e.transpose(out=bk[:, k * 128:(k + 1) * 128], in_=pad[:, j0 + jj, :], identity=ident[:])
                            return r

                        P.op("pe", trs, [padb, cbuf], [bb_])
                        dst = BbT[:, j0:j0 + 2, :, :].rearrange("p a b c -> p (a b c)")
                        if n % 2 == 0:
                            P.op("act", lambda e, bk=bk, dst=dst: e.activation(out=dst, in_=bk[:], func=AF.Copy), [bb_], [matb])
                        else:
                            P.op("dve", lambda e, bk=bk, dst=dst: e.tensor_copy(out=dst, in_=bk[:]), [bb_], [matb])
                    if dump is not None and "DBG" in dump:
                        def dd(name, t, dt):
                            shp = list(t.shape)
                            o = nc.dram_tensor("DBG_" + name, shp, dt, kind="ExternalOutput")
                            P.dma("sp", o.ap(), t[:], reads=[pb, tb, padb, matb, cbuf], writes=[Buf()])
                        for nme in ("r", "fr", "fre", "fim", "cs", "sn"):
                            dd(nme, T[nme], F32)
                        dd("padr", padr, F32)
                        dd("padi", padi, F32)
                        dd("CTr", CTr, BF16)
                        dd("CTi", CTi, BF16)
                        dd("BbT", BbT, BF16)
                    P.emit()
                if stop_after < 2:
                    return
                P = Prog(G)

                Ec = sbt(es, "s_Ec", [128, 4, TS], F32)
                Es = sbt(es, "s_Es", [128, 4, TS], F32)
                tabb = [Buf() for _ in range(4)]
                u1 = sbt(es, "s_u1", [128, TS], F32)
                u1i = sbt(es, "s_u1i", [128, TS], I32)
                hfb = sbt(es, "s_hf", [128, TS], F32)
                u1b = Buf()
                NW = 2
                mtiles = [[sbt(es, "s_m%d_%d" % (k, w), [128, TS], F32) for k in range(4)] for w in range(NW)]
                mb = [[Buf() for k in range(4)] for w in range(NW)]
                btr = [sbt(es, "s_btr%d" % w, [128, TS], F32) for w in range(NW)]
                bti = [sbt(es, "s_bti%d" % w, [128, TS], F32) for w in range(NW)]
                btb = [[Buf(), Buf()] for w in range(NW)]
                str_ = [sbt(es, "s_str%d" % w, [128, TS], F32) for w in range(NW)]
                sti = [sbt(es, "s_sti%d" % w, [128, TS], F32) for w in range(NW)]
                stb = [[Buf(), Buf()] for w in range(NW)]
                ntl = [[sbt(es, "s_n%d_%d" % (k, w), [128, TS], F32) for k in range(4)] for w in range(NW)]
                nb = [[Buf() for k in range(4)] for w in range(NW)]
                sre = [sbt(es, "s_sre%d" % w, [128, TS], BF16) for w in range(NW)]
                sim = [sbt(es, "s_sim%d" % w, [128, TS], BF16) for w in range(NW)]
                sb_ = [[Buf(), Buf()] for w in range(NW)]
                ctmp = sbt(es, "s_ctmp", [128, 2], F32)
                carb = [Buf() for _ in range(NPAIR)]
                yv = [sbt(es, "s_yv%d" % w, [128, TS], F32) for w in range(2)]
                yt = [sbt(es, "s_yt%d" % w, [128, TS], F32) for w in range(2)]
                yb = [Buf() for _ in range(2)]
                pbu = [[pst(es, "s_pbu%d_%d" % (ri, w), [128, TS]) for ri in range(2)] for w in range(2)]
                pbub = [[Buf(), Buf()] for w in range(2)]
                pY = [pst(es, "s_pY%d" % w, [128, TS]) for w in range(2)]
                pYb = [Buf() for _ in range(2)]
                ubs = [[Buf() for _ in range(NT)] for _ in range(KC)]
                step = 0
                for q in range(KC):
                    for jj in range(4):
                        j = 4 * q + jj
                        P.op("dve", lambda e, j=j: e.tensor_scalar(out=u1[:], in0=iota_f[:], scalar1=T["fr"][:, j:j + 1], scalar2=None, op0=ALU.mult), [cbuf, tb, u1b], [u1b])
                        P.op("dve", lambda e: e.tensor_copy(out=u1i[:], in_=u1[:]), [u1b], [u1b])
                        P.op("dve", lambda e: e.tensor_tensor(out=u1[:], in0=u1[:], in1=u1i[:], op=ALU.subtract), [u1b], [u1b])
                        P.op("act", lambda e, jj=jj: e.activation(out=Es[:, jj, :], in_=u1[:], func=AF.Sin, scale=TWO_PI_LO), [u1b], [tabb[jj]])
                        P.op("act", lambda e: e.activation(out=hfb[:], in_=u1[:], func=AF.Sin, scale=PI_LO), [u1b], [u1b])
                        P.op("act", lambda e: e.activation(out=hfb[:], in_=hfb[:], func=AF.Square), [u1b], [u1b])
                        P.op("dve", lambda e, jj=jj: e.tensor_scalar(out=Ec[:, jj, :], in0=hfb[:], scalar1=-2.0, scalar2=1.0, op0=ALU.mult, op1=ALU.add), [u1b], [tabb[jj], u1b])
                    for i in range(NT):
                        tsl = slice(i * TS, (i + 1) * TS)
                        yi = (q * NT + i) % 2
                        for jj in range(4):
                            j = 4 * q + jj
                            w = step % NW
                            w2 = step % 2
                            step += 1
                            ec, es_ = Ec[:, jj, :], Es[:, jj, :]
                            for ri in range(2):
                                P.op("pe", lambda e, ri=ri, j=j, w2=w2, q=q, tsl=tsl: e.matmul(pbu[w2][ri][:], lhsT=BbT[:, j, ri, :], rhs=uT[:, q, tsl], start=True, stop=True), [matb, ubs[q][i]], [pbub[w2][ri]])
                            m = mtiles[w]
                            P.op("dve", lambda e, m=m, w2=w2, ec=ec: e.tensor_tensor(out=m[0][:], in0=pbu[w2][0][:], in1=ec, op=ALU.mult), [pbub[w2][0], tabb[jj]], [mb[w][0]])
                            P.op("dve", lambda e, m=m, w2=w2, es_=es_: e.tensor_tensor(out=m[1][:], in0=pbu[w2][1][:], in1=es_, op=ALU.mult), [pbub[w2][1], tabb[jj]], [mb[w][1]])
                            P.op("dve", lambda e, m=m, w2=w2, ec=ec: e.tensor_tensor(out=m[2][:], in0=pbu[w2][1][:], in1=ec, op=ALU.mult), [pbub[w2][1], tabb[jj]], [mb[w][2]])
                            P.op("dve", lambda e, m=m, w2=w2, es_=es_: e.tensor_tensor(out=m[3][:], in0=pbu[w2][0][:], in1=es_, op=ALU.mult), [pbub[w2][0], tabb[jj]], [mb[w][3]])
                            P.op("pool", lambda e, m=m, w=w: e.tensor_tensor(out=btr[w][:], in0=m[0][:], in1=m[1][:], op=ALU.add), [mb[w][0], mb[w][1]], [btb[w][0]])
                            P.op("pool", lambda e, m=m, w=w: e.tensor_tensor(out=bti[w][:], in0=m[2][:], in1=m[3][:], op=ALU.subtract), [mb[w][2], mb[w][3]], [btb[w][1]])
                            rb = T["r"][:, j:j + 1].to_broadcast([128, TS])
                            P.op("dve", lambda e, w=w, j=j, rb=rb: e.tensor_tensor_scan(out=str_[w][:], data0=rb, data1=btr[w][:], initial=carry[:, j, 0:1], op0=ALU.mult, op1=ALU.add), [btb[w][0], carb[j], tb], [stb[w][0]])
                            P.op("dve", lambda e, w=w, j=j, rb=rb: e.tensor_tensor_scan(out=sti[w][:], data0=rb, data1=bti[w][:], initial=carry[:, j, 1:2], op0=ALU.mult, op1=ALU.add), [btb[w][1], carb[j], tb], [stb[w][1]])
                            nn = ntl[w]
                            P.op("pool", lambda e, nn=nn, w=w, ec=ec: e.tensor_tensor(out=nn[0][:], in0=str_[w][:], in1=ec, op=ALU.mult), [stb[w][0], tabb[jj]], [nb[w][0]])
                            P.op("pool", lambda e, nn=nn, w=w, es_=es_: e.tensor_tensor(out=nn[1][:], in0=sti[w][:], in1=es_, op=ALU.mult), [stb[w][1], tabb[jj]], [nb[w][1]])
                            P.op("pool", lambda e, nn=nn, w=w, es_=es_: e.tensor_tensor(out=nn[2][:], in0=str_[w][:], in1=es_, op=ALU.mult), [stb[w][0], tabb[jj]], [nb[w][2]])
                            P.op("pool", lambda e, nn=nn, w=w, ec=ec: e.tensor_tensor(out=nn[3][:], in0=sti[w][:], in1=ec, op=ALU.mult), [stb[w][1], tabb[jj]], [nb[w][3]])
                            P.op("pool", lambda e, nn=nn, w=w: e.tensor_tensor(out=sre[w][:], in0=nn[0][:], in1=nn[1][:], op=ALU.subtract), [nb[w][0], nb[w][1]], [sb_[w][0]])
                            P.op("pool", lambda e, nn=nn, w=w: e.tensor_tensor(out=sim[w][:], in0=nn[2][:], in1=nn[3][:], op=ALU.add), [nb[w][2], nb[w][3]], [sb_[w][1]])
                            P.op("dve", lambda e, nn=nn, j=j: e.tensor_tensor(out=carry[:, j, 0:1], in0=nn[0][:, TS - 1:TS], in1=nn[1][:, TS - 1:TS], op=ALU.subtract), [nb[w][0], nb[w][1]], [carb[j]])
                            P.op("dve", lambda e, nn=nn, j=j: e.tensor_tensor(out=carry[:, j, 1:2], in0=nn[2][:, TS - 1:TS], in1=nn[3][:, TS - 1:TS], op=ALU.add), [nb[w][2], nb[w][3]], [carb[j]])
                            def ymm(e, j=j, w=w, jj=jj, yi=yi):
                                e.matmul(pY[yi][:], lhsT=CTr[:, j, :], rhs=sre[w][:], start=(jj == 0), stop=False)
                                return e.matmul(pY[yi][:], lhsT=CTi[:, j, :], rhs=sim[w][:], start=False, stop=(jj == 3))

                            P.op("pe", ymm, [matb, sb_[w][0], sb_[w][1]], [pYb[yi]])
                        uv = uT[:, q, tsl]
                        P.op("dve", lambda e, yi=yi, uv=uv, q=q: e.scalar_tensor_tensor(out=yv[yi][:], in0=uv, scalar=dsk[:, q:q + 1], in1=pY[yi][:], op0=ALU.mult, op1=ALU.add), [pYb[yi], ubs[q][i], pb], [yb[yi]])
                        P.op("act", lambda e, yi=yi: e.activation(out=yt[yi][:], in_=yv[yi][:], func=AF.Square), [yb[yi]], [yb[yi]])
                        P.op("dve", lambda e, yi=yi: e.tensor_scalar(out=yt[yi][:], in0=yt[yi][:], scalar1=0.044715, scalar2=1.0, op0=ALU.mult, op1=ALU.add), [yb[yi]], [yb[yi]])
                        P.op("dve", lambda e, yi=yi: e.tensor_tensor(out=yt[yi][:], in0=yt[yi][:], in1=yv[yi][:], op=ALU.mult), [yb[yi]], [yb[yi]])
                        P.op("act", lambda e, yi=yi: e.activation(out=yt[yi][:], in_=yt[yi][:], func=AF.Sigmoid, scale=2.0 * math.sqrt(2.0 / math.pi)), [yb[yi]], [yb[yi]])
                        P.op("dve", lambda e, yi=yi, uv=uv: e.tensor_tensor(out=uv, in0=yt[yi][:], in1=yv[yi][:], op=ALU.mult), [yb[yi]], [ubs[q][i]])
                if UD is not None:
                    P.dma("sp", UD.ap().rearrange("c p t -> p c t"), uT[:], reads=[b for row in ubs for b in row], writes=[Buf()])
                P.emit()

        def phase_A3(uT):
            with contextlib.ExitStack() as es:
                P = Prog(G)
                w_sb = sbt(es, "a3_w", [128, KC, 2 * D], BF16)
                wb = Buf("w")
                load_w_cast(P, lambda rc, c0, c1: w_sb[:, rc, c0:c1], w_glu.ap(), KC, 2 * D, wb)
                xts = [sbt(es, "a3_x%d" % i, [128, KC, TS], F32) for i in range(2)]
                xbs = [[Buf() for _ in range(KC)] for _ in range(2)]
                sg = [sbt(es, "a3_sg%d" % i, [128, TS], F32) for i in range(2)]
                sgb = [Buf() for _ in range(2)]
                pa = [pst(es, "a3_pa%d" % i, [128, TS]) for i in range(2)]
                pab = [Buf() for _ in range(2)]
                pbk = [pst(es, "a3_pb%d" % i, [128, TS]) for i in range(2)]
                pbb = [Buf() for _ in range(2)]
                n = 0
                for i in range(NT):
                    xt, xb = xts[i % 2], xbs[i % 2]
                    tsl = slice(i * TS, (i + 1) * TS)
                    P.dma("sp", xt[:], xtile(xin, i), writes=xb)
                    for fo in range(KC):
                        k = n % 2
                        n += 1

                        def mma(e, fo=fo, k=k, c0=0, bank=pa, tsl=tsl):
                            r = None
                            for kc in range(KC):
                                r = e.matmul(bank[k][:], lhsT=w_sb[:, kc, c0 + fo * 128:c0 + (fo + 1) * 128], rhs=uT[:, kc, tsl], start=(kc == 0), stop=(kc == KC - 1))
                            return r

                        P.op("pe", mma, [wb], [pab[k]])
                        P.op("pe", lambda e, fo=fo, k=k, mma=mma: mma(e, fo, k, D, pbk), [wb], [pbb[k]])
                        P.op("act", lambda e, k=k: e.activation(out=sg[k][:], in_=pbk[k][:], func=AF.Sigmoid), [pbb[k]], [sgb[k]])
                        P.op("dve", lambda e, k=k: e.tensor_tensor(out=sg[k][:], in0=pa[k][:], in1=sg[k][:], op=ALU.mult), [pab[k], sgb[k]], [sgb[k]])
                        P.op("pool", lambda e, k=k, fo=fo, xt=xt: e.tensor_tensor(out=xt[:, fo, :], in0=xt[:, fo, :], in1=sg[k][:], op=ALU.add), [sgb[k], xb[fo]], [xb[fo]])
                    P.dma("sp", xtile(XT, i), xt[:], reads=xb, writes=[Buf()])
                P.emit()

        def phase_FFN(l, last):
            with contextlib.ExitStack() as es:
                P = Prog(G)
                wup = sbt(es, "f_wup", [128, KC, 2 * DFF], BF16)
                wdn = sbt(es, "f_wdn", [128, FC, D], BF16)
                wub, wdb = Buf("wup"), Buf("wdn")
                load_w_cast(P, lambda rc, c0, c1: wup[:, rc, c0:c1], w_up.ap()[l], KC, 2 * DFF, wub)
                load_w_cast(P, lambda rc, c0, c1: wdn[:, rc, c0:c1], w_dn.ap()[l], FC, D, wdb)
                cw_sb = sbt(es, "f_cw", [128, FC, 3], F32)
                cb_sb = sbt(es, "f_cb", [128, FC], F32)
                cpb = Buf()
                P.dma("sp", cw_sb[:], cw.ap()[:, l], writes=[cpb])
                P.dma("sp", cb_sb[:], cb.ap()[:, l], writes=[cpb])
                xt = sbt(es, "f_x", [128, KC, TS], F32)
                xb = [Buf() for _ in range(KC)]
                aT = sbt(es, "f_aT", [128, FC, TS], BF16)
                ab = [Buf() for _ in range(FC)]
                sq = aT[:, 0:KC, :].rearrange("p c t -> p (c t)")
                rstd = sbt(es, "f_rstd", [128, TS], F32)
                rsb = Buf()
                ht = sbt(es, "f_h", [128, KC, TS], BF16)
                hb = Buf()
                halo = sbt(es, "f_halo", [128, FC, 2], F32)
                hlb = [Buf() for _ in range(FC)]
                NG = 3
                gbuf = [sbt(es, "f_g%d" % i, [128, TS + 2], F32) for i in range(NG)]
                gbb = [Buf() for _ in range(NG)]
                a0 = [sbt(es, "f_a%d" % i, [128, TS], F32) for i in range(NG)]
                a0b = [Buf() for _ in range(NG)]
                ssum = pst(es, "f_ss", [128, TS])
                ssb = Buf()
                pg = [pst(es, "f_pg%d" % i, [128, TS]) for i in range(2)]
                pgb = [Buf() for _ in range(2)]
                pu = [pst(es, "f_pu%d" % i, [128, TS]) for i in range(2)]
                pub = [Buf() for _ in range(2)]
                pd = [pst(es, "f_pd%d" % i, [128, TS]) for i in range(2)]
                pdb = [Buf() for _ in range(2)]
                P.op("pool", lambda e: e.memset(halo[:], 0.0), [], hlb)
                n = 0
                for i in range(NT):
                    P.dma("sp", xt[:], xtile(XT, i), writes=xb)
                    P.op("act", lambda e: e.activation(out=sq, in_=xt[:].rearrange("p c t -> p (c t)"), func=AF.Square), xb, ab[0:KC])

                    def mmss(e):
                        r = None
                        for kc in range(KC):
                            r = e.matmul(ssum[:], lhsT=ones_bf[:], rhs=aT[:, kc, :], start=(kc == 0), stop=(kc == KC - 1))
                        return r

                    P.op("pe", mmss, ab[0:KC], [ssb])
                    P.op("act", lambda e: e.activation(out=rstd[:], in_=ssum[:], func=AF.Sqrt, scale=1.0 / D, bias=1e-6), [ssb], [rsb])
                    P.op("dve", lambda e: e.reciprocal(out=rstd[:], in_=rstd[:]), [rsb], [rsb])
                    for kc in range(KC):
                        P.op("dve", lambda e, kc=kc: e.scalar_tensor_tensor(out=ht[:, kc, :], in0=xt[:, kc, :], scalar=gam_sb[:, 1 + 3 * l, kc:kc + 1], in1=rstd[:], op0=ALU.mult, op1=ALU.mult), [xb[kc], rsb], [hb])
                    for fc in range(FC):
                        k = n % 2
                        g3 = n % NG
                        n += 1

                        def mmg(e, fc=fc, k=k, c0=0, bank=pg):
                            r = None
                            for kc in range(KC):
                                r = e.matmul(bank[k][:], lhsT=wup[:, kc, c0 + fc * 128:c0 + (fc + 1) * 128], rhs=ht[:, kc, :], start=(kc == 0), stop=(kc == KC - 1))
                            return r

                        P.op("pe", mmg, [wub, hb], [pgb[k]])
                        P.op("pe", lambda e, fc=fc, k=k: mmg(e, fc, k, DFF, pu), [wub, hb], [pub[k]])
                        gb = gbuf[g3]
                        P.op("pool", lambda e, gb=gb, fc=fc: e.tensor_copy(out=gb[:, 0:2], in_=halo[:, fc, :]), [hlb[fc]], [gbb[g3]])
                        P.op("act", lambda e, gb=gb, k=k: e.activation(out=gb[:, 2:TS + 2], in_=pg[k][:], func=AF.Copy), [pgb[k], gbb[g3]], [gbb[g3]])
                        P.op("pool", lambda e, gb=gb, fc=fc: e.tensor_copy(out=halo[:, fc, :], in_=gb[:, TS:TS + 2]), [gbb[g3]], [hlb[fc]])
                        P.op("act", lambda e, k=k, g3=g3, fc=fc: e.activation(out=a0[g3][:], in_=pg[k][:], func=AF.Identity, scale=cw_sb[:, fc, 2:3], bias=cb_sb[:, fc:fc + 1]), [pgb[k], cpb], [a0b[g3]])
                        P.op("dve", lambda e, gb=gb, g3=g3, fc=fc: e.scalar_tensor_tensor(out=a0[g3][:], in0=gb[:, 1:TS + 1], scalar=cw_sb[:, fc, 1:2], in1=a0[g3][:], op0=ALU.mult, op1=ALU.add), [gbb[g3], a0b[g3], cpb], [a0b[g3]])
                        P.op("dve", lambda e, gb=gb, g3=g3, fc=fc: e.scalar_tensor_tensor(out=a0[g3][:], in0=gb[:, 0:TS], scalar=cw_sb[:, fc, 0:1], in1=a0[g3][:], op0=ALU.mult, op1=ALU.add), [gbb[g3], a0b[g3], cpb], [a0b[g3]])
                        P.op("act", lambda e, g3=g3: e.activation(out=a0[g3][:], in_=a0[g3][:], func=AF.Silu), [a0b[g3]], [a0b[g3]])
                        P.op("dve", lambda e, g3=g3, k=k, fc=fc: e.tensor_tensor(out=aT[:, fc, :], in0=a0[g3][:], in1=pu[k][:], op=ALU.mult), [a0b[g3], pub[k], ssb], [ab[fc]])
                    for fo in range(KC):
                        k = fo % 2

                        def mmd(e, fo=fo, k=k):
                            r = None
                            for fc in range(FC):
                                r = e.matmul(pd[k][:], lhsT=wdn[:, fc, fo * 128:(fo + 1) * 128], rhs=aT[:, fc, :], start=(fc == 0), stop=(fc == FC - 1))
                            return r

                        P.op("pe", mmd, [wdb] + ab, [pdb[k]])
                        P.op("dve", lambda e, fo=fo, k=k: e.tensor_tensor(out=xt[:, fo, :], in0=pd[k][:], in1=xt[:, fo, :], op=ALU.add), [pdb[k], xb[fo]], [xb[fo]])
                    if not last:
                        P.dma("sp", xtile(XT, i), xt[:], reads=xb, writes=[Buf()])
                    else:
                        P.op("act", lambda e: e.activation(out=sq, in_=xt[:].rearrange("p c t -> p (c t)"), func=AF.Square), xb, ab[0:KC])
                        P.op("pe", mmss, ab[0:KC], [ssb])
                        P.op("act", lambda e: e.activation(out=rstd[:], in_=ssum[:], func=AF.Sqrt, scale=1.0 / D, bias=1e-6), [ssb], [rsb])
                        P.op("dve", lambda e: e.reciprocal(out=rstd[:], in_=rstd[:]), [rsb], [rsb])
                        for kc in range(KC):
                            P.op("dve", lambda e, kc=kc: e.scalar_tensor_tensor(out=xt[:, kc, :], in0=xt[:, kc, :], scalar=gam_sb[:, 5, kc:kc + 1], in1=rstd[:], op0=ALU.mult, op1=ALU.mult), [xb[kc], rsb], [xb[kc]])
                        P.dma("sp", xtile(outT, i), xt[:], reads=xb, writes=[Buf()])
                P.emit()

        def phase_CD():
            with contextlib.ExitStack() as es:
                P = Prog(G)
                kvw = sbt(es, "c_kvw", [128, KC, 2 * D], BF16)
                wq = sbt(es, "c_wq", [128, KC, D], BF16)
                kvb, wqb = Buf(), Buf()
                load_w_cast(P, lambda rc, c0, c1: kvw[:, rc, c0:c1], kv_w.ap(), KC, 2 * D, kvb)
                load_w_cast(P, lambda rc, c0, c1: wq[:, rc, c0:c1], w_q.ap(), KC, D, wqb)
                xts = [sbt(es, "c_x%d" % i, [128, KC, TS], F32) for i in range(2)]
                xbs = [Buf() for _ in range(2)]
                sq = sbt(es, "c_sq", [128, KC * TS], BF16)
                sqb = Buf()
                rstd = sbt(es, "c_rstd", [128, TS], F32)
                rsb = Buf()
                hk = sbt(es, "c_hk", [128, KC, TS], BF16)
                hq = sbt(es, "c_hq", [128, KC, TS], BF16)
                hkb, hqb = Buf(), Buf()
                kt = [sbt(es, "c_kt%d" % i, [128, KC, TS], BF16) for i in range(2)]
                qt = [sbt(es, "c_qt%d" % i, [128, KC, TS], BF16) for i in range(2)]
                vt = [sbt(es, "c_vt%d" % i, [128, 4, D], BF16) for i in range(2)]
                ktb = [Buf() for _ in range(2)]
                qtb = [Buf() for _ in range(2)]
                vtb = [Buf() for _ in range(2)]
                ssum = pst(es, "c_ss", [128, TS])
                ssb = Buf()
                banks = [pst(es, "c_b%d" % i, [128, TS]) for i in range(6)]
                bbs = [Buf() for _ in range(6)]
                n = 0

                def evac(bk, bb, dst, dstb, scale=None):
                    nonlocal n
                    n += 1
                    if n % 2 == 0:
                        if scale is None:
                            P.op("act", lambda e: e.activation(out=dst, in_=bk[:], func=AF.Copy), [bb], [dstb])
                        else:
                            P.op("act", lambda e: e.activation(out=dst, in_=bk[:], func=AF.Copy, scale=scale), [bb], [dstb])
                    else:
                        if scale is None:
                            P.op("dve", lambda e: e.tensor_copy(out=dst, in_=bk[:]), [bb], [dstb])
                        else:
                            P.op("dve", lambda e: e.tensor_scalar(out=dst, in0=bk[:], scalar1=scale, scalar2=None, op0=ALU.mult), [bb], [dstb])

                nb = 0
                for i in range(NT):
                    xt, xb = xts[i % 2], xbs[i % 2]
                    s = i % 2
                    P.dma("sp", xt[:], xtile(XT, i), writes=[xb])
                    rms_rstd(P, xt, xb, sq[:], sqb, ssum[:], ssb, rstd[:], rsb)
                    rms_apply(P, xt, xb, rstd[:], rsb, 2, hk, hkb)
                    rms_apply(P, xt, xb, rstd[:], rsb, 3, hq, hqb)
                    for fo in range(KC):
                        bk, bb = banks[nb % 6], bbs[nb % 6]
                        nb += 1

                        def mmk(e, fo=fo, bk=bk):
                            r = None
                            for kc in range(KC):
                                r = e.matmul(bk[:], lhsT=kvw[:, kc, fo * 128:(fo + 1) * 128], rhs=hk[:, kc, :], start=(kc == 0), stop=(kc == KC - 1))
                            return r

                        P.op("pe", mmk, [kvb, hkb], [bb])
                        evac(bk, bb, kt[s][:, fo, :], ktb[s])
                    for fo in range(KC):
                        bk, bb = banks[nb % 6], bbs[nb % 6]
                        nb += 1

                        def mmq(e, fo=fo, bk=bk):
                            r = None
                            for kc in range(KC):
                                r = e.matmul(bk[:], lhsT=wq[:, kc, fo * 128:(fo + 1) * 128], rhs=hq[:, kc, :], start=(kc == 0), stop=(kc == KC - 1))
                            return r

                        P.op("pe", mmq, [wqb, hqb], [bb])
                        evac(bk, bb, qt[s][:, fo, :], qtb[s], scale=0.125)
                    for blk in range(4):
                        for half in range(2):
                            bk, bb = banks[nb % 6], bbs[nb % 6]
                            nb += 1

                            def mmv(e, blk=blk, half=half, bk=bk):
                                r = None
                                for kc in range(KC):
                                    r = e.matmul(bk[:], lhsT=hk[:, kc, blk * 128:(blk + 1) * 128], rhs=kvw[:, kc, D + half * 512:D + (half + 1) * 512], start=(kc == 0), stop=(kc == KC - 1))
                                return r

                            P.op("pe", mmv, [kvb, hkb], [bb])
                            evac(bk, bb, vt[s][:, blk, half * 512:(half + 1) * 512], vtb[s])
                    P.dma("sp", xtile(KT, i), kt[s][:], reads=[ktb[s]], writes=[Buf()])
                    P.dma("sp", xtile(QT, i), qt[s][:], reads=[qtb[s]], writes=[Buf()])
                    P.dma("sp", VS.ap()[i * 4:(i + 1) * 4].rearrange("b p d -> p b d"), vt[s][:], reads=[vtb[s]], writes=[Buf()])
                P.emit()

        def phase_D2(OT):
            with contextlib.ExitStack() as es:
                P = Prog(G)
                NB = L // 128
                tri = sbt(es, "d_tri", [128, 128], BF16)
                cb_ = Buf()

                def mk_tri(e):
                    e.memset(tri[:], 1.0)
                    return e.affine_select(out=tri[:], in_=tri[:], pattern=[[-1, 128]], compare_op=ALU.is_ge, fill=0.0, base=0, channel_multiplier=1)

                P.op("pool", mk_tri, [], [cb_])
                ktc = [sbt(es, "d_kt%d" % i, [128, L], BF16) for i in range(2)]
                qtc = [sbt(es, "d_qt%d" % i, [128, L], BF16) for i in range(2)]
                vc = [sbt(es, "d_v%d" % i, [128, NB, 128], BF16) for i in range(2)]
                lb = [Buf() for _ in range(2)]
                NW = 2
                et = [sbt(es, "d_e%d" % i, [128, 2, TS], F32) for i in range(NW)]
                eb = [Buf() for _ in range(NW)]
                lt = [sbt(es, "d_l%d" % i, [128, 2, TS], BF16) for i in range(NW)]
                ltb = [Buf() for _ in range(NW)]
                lacc = [sbt(es, "d_la%d" % i, [128, 2, TS], BF16) for i in range(2)]
                lab = [Buf() for _ in range(2)]
                gt = [sbt(es, "d_g%d" % i, [128, 2, TS], F32) for i in range(NW)]
                gtb = [Buf() for _ in range(NW)]
                wt = [sbt(es, "d_w%d" % i, [128, 2, TS], BF16) for i in range(NW)]
                wtb = [Buf() for _ in range(NW)]
                pz = [pst(es, "d_pz%d" % i, [128, 2, TS]) for i in range(2)]
                pzb = [Buf() for _ in range(2)]
                pr = [pst(es, "d_pr%d" % i, [128, 2, TS]) for i in range(1)]
                prb = [Buf() for _ in range(1)]
                po = pst(es, "d_po", [128, 2, TS])
                pob = Buf()
                otb = Buf()
                step = 0
                for c in range(KC):
                    s = c % 2
                    P.dma("sp", ktc[s][:], KT.ap()[c], writes=[lb[s]])
                    P.dma("sp", qtc[s][:], QT.ap()[c], writes=[lb[s]])
                    for v4 in range(4):
                        P.dma("sp", vc[s][:, v4 * 8:(v4 + 1) * 8, :], VS.ap()[v4 * 8:(v4 + 1) * 8, :, c * 128:(c + 1) * 128].rearrange("b p d -> p b d"), writes=[lb[s]])
                    for qi in range(NT):
                        qsl = slice(qi * TS, (qi + 1) * TS)
                        P.op("pool", lambda e: e.memset(lacc[0][:], 0.0), [], [lab[0]])
                        la = 0
                        kbs = list(range(4 * qi + 3, -1, -1))
                        for n_, kb in enumerate(kbs):
                            w = step % NW
                            z = step % 2
                            step += 1
                            ksl = slice(kb * 128, (kb + 1) * 128)

                            def mmz(e, s=s, z=z, ksl=ksl, qsl=qsl):
                                e.matmul(pz[z][:, 0, :], lhsT=ktc[s][0:64, ksl], rhs=qtc[s][0:64, qsl], start=True, stop=True)
                                return e.matmul(pz[z][:, 1, :], lhsT=ktc[s][64:128, ksl], rhs=qtc[s][64:128, qsl], start=True, stop=True)

                            P.op("pe", mmz, [lb[s]], [pzb[z]])
                            P.op("act", lambda e, w=w, z=z: e.activation(out=et[w][:], in_=pz[z][:], func=AF.Exp), [pzb[z]], [eb[w]])
                            if kb >= 4 * qi:
                                base = qi * TS - kb * 128
                                P.op("pool", lambda e, w=w, base=base: e.affine_select(out=et[w][:], in_=et[w][:], pattern=[[0, 2], [1, TS]], compare_op=ALU.is_gt, fill=0.0, base=base, channel_multiplier=-1), [eb[w]], [eb[w]])
                            P.op("act", lambda e, w=w: e.activation(out=lt[w][:], in_=et[w][:], func=AF.Ln, bias=1.0, scale=1.0), [eb[w]], [ltb[w]])

                            def mmr(e, w=w, la=la):
                                r = None
                                for hh in range(2):
                                    e.matmul(pr[0][:, hh, :], lhsT=tri[:], rhs=lt[w][:, hh, :], start=True, stop=False)
                                    r = e.matmul(pr[0][:, hh, :], lhsT=ones_bf[:], rhs=lacc[la][:, hh, :], start=False, stop=True)
                                return r

                            P.op("pe", mmr, [cb_, ltb[w], lab[la]], [prb[0]])
                            if n_ != len(kbs) - 1:
                                P.op("pool", lambda e, w=w, la=la: e.tensor_tensor(out=lacc[1 - la][:], in0=lacc[la][:], in1=lt[w][:], op=ALU.add), [lab[la], ltb[w]], [lab[1 - la]])
                            la_next = 1 - la
                            P.op("act", lambda e, w=w: e.activation(out=gt[w][:], in_=pr[0][:], func=AF.Exp, scale=-1.0), [prb[0]], [gtb[w]])
                            P.op("dve", lambda e, w=w: e.tensor_tensor(out=wt[w][:], in0=et[w][:], in1=gt[w][:], op=ALU.mult), [eb[w], gtb[w]], [wtb[w]])

                            def mmo(e, s=s, w=w, kb=kb, first=(n_ == 0), lastk=(n_ == len(kbs) - 1)):
                                e.matmul(po[0:64, 0, :], lhsT=vc[s][:, kb, 0:64], rhs=wt[w][:, 0, :], start=first, stop=lastk)
                                return e.matmul(po[64:128, 1, :], lhsT=vc[s][:, kb, 64:128], rhs=wt[w][:, 1, :], start=first, stop=lastk)

                            P.op("pe", mmo, [lb[s], wtb[w]], [pob])
                            la = la_next
                        P.op("act", lambda e, c=c, qsl=qsl: e.activation(out=OT[0:64, c, qsl], in_=po[0:64, 0, :], func=AF.Copy), [pob], [otb])
                        P.op("dve", lambda e, c=c, qsl=qsl: e.tensor_copy(out=OT[64:128, c, qsl], in_=po[64:128, 1, :]), [pob], [otb])
                P.emit()

        def phase_D3(OT):
            with contextlib.ExitStack() as es:
                P = Prog(G)
                w_sb = sbt(es, "o_w", [128, KC, D], BF16)
                wb = Buf()
                load_w_cast(P, lambda rc, c0, c1: w_sb[:, rc, c0:c1], w_o.ap(), KC, D, wb)
                xts = [sbt(es, "o_x%d" % i, [128, KC, TS], F32) for i in range(2)]
                xbs = [[Buf() for _ in range(KC)] for _ in range(2)]
                banks = [pst(es, "o_b%d" % i, [128, TS]) for i in range(4)]
                bbs = [Buf() for _ in range(4)]
                for i in range(NT):
                    xt, xb = xts[i % 2], xbs[i % 2]
                    tsl = slice(i * TS, (i + 1) * TS)
                    P.dma("sp", xt[:], xtile(XT, i), writes=xb)
                    for fo in range(KC):
                        bk, bb = banks[fo % 4], bbs[fo % 4]

                        def mm(e, fo=fo, bk=bk, tsl=tsl):
                            r = None
                            for kc in range(KC):
                                r = e.matmul(bk[:], lhsT=w_sb[:, kc, fo * 128:(fo + 1) * 128], rhs=OT[:, kc, tsl], start=(kc == 0), stop=(kc == KC - 1))
                            return r

                        P.op("pe", mm, [wb], [bb])
                        P.op("dve", lambda e, fo=fo, bk=bk, xt=xt: e.tensor_tensor(out=xt[:, fo, :], in0=bk[:], in1=xt[:, fo, :], op=ALU.add), [bb, xb[fo]], [xb[fo]])
                    P.dma("sp", xtile(XT, i), xt[:], reads=xb, writes=[Buf()])
                P.emit()

        init_consts()
        with contextlib.ExitStack() as esA:
            uT = sbt(esA, "uT", [128, KC, L], BF16)
            phase_A1(uT)
            if stop_after >= 1.5:
                phase_A2(uT)
            if stop_after >= 3:
                phase_A3(uT)
        if stop_after >= 4:
            phase_FFN(0, False)
        if stop_after >= 5:
            phase_CD()
        if stop_after >= 6:
            with contextlib.ExitStack() as esD:
                OT = sbt(esD, "OT", [128, KC, L], BF16)
                phase_D2(OT)
                if stop_after >= 7:
                    phase_D3(OT)
        if stop_after >= 8:
            phase_FFN(1, True)
    return nc


def prep_shared(inp):
    f = np.float32
    c = lambda a: np.ascontiguousarray(a, dtype=f)
    g6 = np.stack([inp["norm_mix"][0], inp["norm_ffn"][0], inp["norm_kv"], inp["norm_mix"][1], inp["norm_ffn"][1], inp["norm_final"]], 0)
    gam = c(g6.reshape(6, KC, 128).transpose(2, 0, 1))

    def pairs2(a):
        return c(a.reshape(32, 2, 64).transpose(1, 2, 0).reshape(128, 32))

    ldt = np.repeat(inp["ssm_log_dt"][0].reshape(32, 2, 1), 64, axis=2)
    d = {
        "gam": gam,
        "w_in": c(inp["ssm_w_in"][0]),
        "w_glu": c(inp["ssm_w_glu"][0]),
        "kv_w": c(inp["kv_w"]),
        "w_q": c(inp["attn_w_q"][0]),
        "w_o": c(inp["attn_w_o"][0]),
        "w_up": c(inp["ffn_w_up"]),
        "w_dn": c(inp["ffn_w_down"]),
        "cw": c(inp["ffn_conv_w"][:, :, 0, :].reshape(2, 3, FC, 128).transpose(3, 0, 2, 1)),
        "cb": c(inp["ffn_conv_b"].reshape(2, FC, 128).transpose(2, 0, 1)),
        "s_are": pairs2(inp["ssm_a_re"][0]),
        "s_aim": pairs2(inp["ssm_a_im"][0]),
        "s_ldt": c(ldt.transpose(1, 2, 0).reshape(128, 32)),
        "s_bre": c(inp["ssm_b_re"][0].reshape(32, 2, 64, 16).transpose(1, 2, 0, 3).reshape(128, 32, 16)),
        "s_bim": c(inp["ssm_b_im"][0].reshape(32, 2, 64, 16).transpose(1, 2, 0, 3).reshape(128, 32, 16)),
        "s_cre": c(inp["ssm_c_re"][0].reshape(32, 2, 16, 64).transpose(1, 3, 0, 2).reshape(128, 32, 16)),
        "s_cim": c(inp["ssm_c_im"][0].reshape(32, 2, 16, 64).transpose(1, 3, 0, 2).reshape(128, 32, 16)),
        "s_d": c(inp["ssm_d"][0].reshape(KC, 128).T),
    }
    return d


def kernel(**inputs):
    inp = {k: np.asarray(v) for k, v in inputs.items()}
    x = inp["x"]
    B = x.shape[0]
    shared = prep_shared(inp)
    in_maps = []
    for b in range(B):
        m = dict(shared)
        m["xin"] = np.ascontiguousarray(x[b].T).reshape(KC, 128, L)
        in_maps.append(m)
    nc = build()
    res = run_bass_kernel_spmd(nc, in_maps, core_ids=list(range(B)))
    out = np.empty((B, L, D), np.float32)
    for b in range(B):
        out[b] = res.results[b]["outT"].reshape(D, L).T
    return out
```

```python
import contextlib
import math
import numpy as np
import concourse.bass as bass
import concourse.mybir as mybir
from concourse.bass_utils import run_bass_kernel_spmd

F32 = mybir.dt.float32
BF16 = mybir.dt.bfloat16
I32 = mybir.dt.int32
AF = mybir.ActivationFunctionType
ALU = mybir.AluOpType

L = 4096
D = 1024
KC = 8
TS = 512
NT = L // TS
DFF = 2816
FC = DFF // 128
NPAIR = 32
TWO_PI_LO = 6.283185
PI_LO = 3.1415925

ENGS = ("pe", "act", "dve", "pool", "sp")
DMA_RING = 8
SEM_LIMIT = 4000


class Buf:
    __slots__ = ("name", "w", "r")

    def __init__(self, name=""):
        self.name = name
        self.w = None
        self.r = []


class Op:
    __slots__ = ("eng", "fn", "deps", "dma", "need", "sem", "val", "idx", "ring_prev")

    def __init__(self, eng, fn, dma):
        self.eng = eng
        self.fn = fn
        self.dma = dma
        self.deps = []
        self.need = False
        self.sem = None
        self.val = 0
        self.ring_prev = None


class Glob:
    def __init__(self, nc, es):
        self.nc = nc
        nsem = {"pe": 2, "act": 2, "dve": 3, "pool": 1, "sp": 1}
        self.esems = {e: [es.enter_context(nc.semaphore("c_%s%d" % (e, i))) for i in range(nsem[e])] for e in ENGS}
        self.cur = {e: 0 for e in ENGS}
        self.rings = {
            e: [es.enter_context(nc.semaphore("d_%s%d" % (e, i))) for i in range(DMA_RING)]
            for e in ("sp", "pool")
        }
        self.cnt = {e: 0 for e in ENGS}
        self.dcnt = {e: 0 for e in self.rings}
        self.ring_last = {}
        self.seen = {e: {} for e in ENGS}
        self.simvals = {}
        self.simulate = False


class Prog:
    def __init__(self, g):
        self.g = g
        self.nc = g.nc
        self.ops = []
        self.touched = {}

    def op(self, eng, fn, reads=(), writes=(), dma=False):
        o = Op(eng, fn, dma)
        o.idx = len(self.ops)
        for b in reads:
            self.touched[id(b)] = b
        for b in writes:
            self.touched[id(b)] = b
        deps = set()
        for b in reads:
            if b.w is not None:
                deps.add(b.w)
        for b in writes:
            if b.w is not None:
                deps.add(b.w)
            for r in b.r:
                deps.add(r)
        deps.discard(o.idx)
        o.deps = sorted(deps)
        for b in reads:
            b.r.append(o.idx)
        for b in writes:
            b.w = o.idx
            b.r = []
        self.ops.append(o)
        return o

    def dma(self, q, out, in_, reads=(), writes=()):
        return self.op(q, lambda e: e.dma_start(out=out, in_=in_), reads, writes, dma=True)

    def emit(self):
        g = self.g
        nc = self.nc
        ops = self.ops
        for o in ops:
            for d in o.deps:
                dop = ops[d]
                if dop.eng == "pe" and o.eng == "pe" and not dop.dma and not o.dma:
                    continue
                dop.need = True
        for o in ops:
            if o.dma:
                k = g.dcnt[o.eng]
                g.dcnt[o.eng] += 1
                slot = k % DMA_RING
                o.sem = g.rings[o.eng][slot]
                o.val = 16 * (k // DMA_RING + 1)
                o.ring_prev = g.ring_last.get((o.eng, slot))
                g.ring_last[(o.eng, slot)] = (o.sem, o.val)
            elif o.need:
                if g.cnt[o.eng] >= SEM_LIMIT:
                    g.cur[o.eng] += 1
                    g.cnt[o.eng] = 0
                g.cnt[o.eng] += 1
                o.sem = g.esems[o.eng][g.cur[o.eng]]
                o.val = g.cnt[o.eng]
        per = {e: [o for o in ops if o.eng == e] for e in ENGS}
        ring_final = dict(g.ring_last)
        if getattr(g, "simulate", False):
            sv = g.simvals
            pos = {e: 0 for e in ENGS}
            done = 0
            total = len(ops)
            while done < total:
                prog = False
                for e in ENGS:
                    while pos[e] < len(per[e]):
                        o = per[e][pos[e]]
                        ok = True
                        for d in o.deps:
                            dop = ops[d]
                            if dop.eng == "pe" and e == "pe" and not dop.dma and not o.dma:
                                continue
                            if sv.get(id(dop.sem), 0) < dop.val:
                                ok = False
                                break
                        if ok and o.dma and o.ring_prev is not None:
                            s_, v_ = o.ring_prev
                            if sv.get(id(s_), 0) < v_:
                                ok = False
                        if not ok:
                            break
                        if o.dma:
                            sv[id(o.sem)] = sv.get(id(o.sem), 0) + 16
                            assert sv[id(o.sem)] == o.val, ("dma val", e, pos[e])
                        elif o.need:
                            sv[id(o.sem)] = sv.get(id(o.sem), 0) + 1
                            assert sv[id(o.sem)] == o.val, ("cnt val", e, pos[e])
                        pos[e] += 1
                        done += 1
                        prog = True
                if not prog:
                    raise RuntimeError("DEADLOCK in schedule: " + str({e: (pos[e], len(per[e])) for e in ENGS}))
            print("sim ok: ops", total, "counters", {e: g.cnt[e] for e in ENGS}, flush=True)

        def body(ename, eng):
            seen = g.seen[ename]

            def wait(sem, val):
                key = id(sem)
                if seen.get(key, 0) >= val:
                    return
                seen[key] = val
                eng.wait_ge(sem, val)

            for o in per[ename]:
                want = {}
                for d in o.deps:
                    dop = ops[d]
                    if dop.eng == "pe" and ename == "pe" and not dop.dma and not o.dma:
                        continue
                    key = id(dop.sem)
                    if key not in want or want[key][1] < dop.val:
                        want[key] = (dop.sem, dop.val)
                if o.dma and o.ring_prev is not None:
                    s, v = o.ring_prev
                    key = id(s)
                    if key not in want or want[key][1] < v:
                        want[key] = (s, v)
                for s, v in want.values():
                    wait(s, v)
                ins = o.fn(eng)
                if o.dma:
                    ins.then_inc(o.sem, 16)
                elif o.need:
                    ins.then_inc(o.sem, 1)
            if ename == "sp":
                for (s, v) in ring_final.values():
                    wait(s, v)

        with nc.Block() as block:

            @block.tensor
            def _(e):
                body("pe", e)

            @block.scalar
            def _(e):
                body("act", e)

            @block.vector
            def _(e):
                body("dve", e)

            @block.gpsimd
            def _(e):
                body("pool", e)

            @block.sync
            def _(e):
                body("sp", e)

        for b in self.touched.values():
            b.w = None
            b.r = []


class Rot:
    def __init__(self, items):
        self.items = items
        self.i = 0

    def next(self):
        it = self.items[self.i % len(self.items)]
        self.i += 1
        return it


def build(stop_after=99, dump=None):
    nc = bass.Bass("TRN2", target_bir_lowering=False)

    def din(name, shape, dt=F32):
        return nc.dram_tensor(name, shape, dt, kind="ExternalInput")

    def dscr(name, shape, dt):
        kind = "ExternalOutput" if (dump is not None and name in dump) else "Internal"
        return nc.dram_tensor(name, shape, dt, kind=kind)

    xin = din("xin", [KC, 128, L])
    gam = din("gam", [128, 6, KC])
    w_in = din("w_in", [D, D])
    w_glu = din("w_glu", [D, 2 * D])
    kv_w = din("kv_w", [D, 2 * D])
    w_q = din("w_q", [D, D])
    w_o = din("w_o", [D, D])
    w_up = din("w_up", [2, D, 2 * DFF])
    w_dn = din("w_dn", [2, DFF, D])
    cw = din("cw", [128, 2, FC, 3])
    cb = din("cb", [128, 2, FC])
    s_are = din("s_are", [128, NPAIR])
    s_aim = din("s_aim", [128, NPAIR])
    s_ldt = din("s_ldt", [128, NPAIR])
    s_bre = din("s_bre", [128, NPAIR, 16])
    s_bim = din("s_bim", [128, NPAIR, 16])
    s_cre = din("s_cre", [128, NPAIR, 16])
    s_cim = din("s_cim", [128, NPAIR, 16])
    s_d = din("s_d", [128, KC])
    outT = nc.dram_tensor("outT", [KC, 128, L], F32, kind="ExternalOutput")

    XT = dscr("XT", [KC, 128, L], F32)
    KT = dscr("KTs", [KC, 128, L], BF16)
    QT = dscr("QTs", [KC, 128, L], BF16)
    VS = dscr("VSs", [L // 128, 128, D], BF16)
    UD = dscr("UD", [KC, 128, L], BF16) if (dump and "UD" in dump) else None

    def xtile(dr, i):
        return dr.ap()[:, :, i * TS:(i + 1) * TS].rearrange("c p t -> p c t")

    with contextlib.ExitStack() as top:
        G = Glob(nc, top)

        uniq = [0]

        def sbt(es, name, shape, dt):
            uniq[0] += 1
            return es.enter_context(nc.sbuf_tensor("S%d_%s" % (uniq[0], name), shape, dt))

        def pst(es, name, shape, dt=F32):
            uniq[0] += 1
            return es.enter_context(nc.psum_tensor("P%d_%s" % (uniq[0], name), shape, dt))

        ones_bf = sbt(top, "ones_bf", [128, 128], BF16)
        gam_sb = sbt(top, "gam_sb", [128, 6, KC], F32)

        def load_w_cast(P, dst_fn, src2d, nrows_chunks, ncols, wbuf):
            for rc in range(nrows_chunks):
                c0 = 0
                while c0 < ncols:
                    c1 = min(ncols, c0 + 2048)
                    P.dma("pool", dst_fn(rc, c0, c1), src2d[rc * 128:(rc + 1) * 128, c0:c1], writes=[wbuf])
                    c0 = c1

        def rms_rstd(P, xt, xb, sq, sqb, ssum, ssb, rstd, rsb):
            P.op("act", lambda e: e.activation(out=sq, in_=xt[:].rearrange("p c t -> p (c t)"), func=AF.Square), [xb], [sqb])

            def mm(e):
                r = None
                for kc in range(KC):
                    r = e.matmul(ssum, lhsT=ones_bf[:], rhs=sq[:, kc * TS:(kc + 1) * TS], start=(kc == 0), stop=(kc == KC - 1))
                return r

            P.op("pe", mm, [sqb], [ssb])
            P.op("act", lambda e: e.activation(out=rstd, in_=ssum, func=AF.Sqrt, scale=1.0 / D, bias=1e-6), [ssb], [rsb])
            P.op("dve", lambda e: e.reciprocal(out=rstd, in_=rstd), [rsb], [rsb])

        def rms_apply(P, xt, xb, rstd, rsb, gi, ht, hb):
            for kc in range(KC):
                P.op("dve", lambda e, kc=kc: e.scalar_tensor_tensor(out=ht[:, kc, :], in0=xt[:, kc, :], scalar=gam_sb[:, gi, kc:kc + 1], in1=rstd, op0=ALU.mult, op1=ALU.mult), [xb, rsb], [hb])

        def init_consts():
            with contextlib.ExitStack() as es:
                P = Prog(G)
                P.op("pool", lambda e: e.memset(ones_bf[:], 1.0), [], [Buf()])
                P.dma("sp", gam_sb[:], gam.ap(), writes=[Buf()])
                P.emit()

        def phase_A1(uT):
            with contextlib.ExitStack() as es:
                P = Prog(G)
                w_sb = sbt(es, "a1_w", [128, KC, D], BF16)
                wb = Buf("w")
                load_w_cast(P, lambda rc, c0, c1: w_sb[:, rc, c0:c1], w_in.ap(), KC, D, wb)
                xts = [sbt(es, "a1_x%d" % i, [128, KC, TS], F32) for i in range(2)]
                xbs = [Buf() for _ in range(2)]
                sq = sbt(es, "a1_sq", [128, KC * TS], BF16)
                sqb = Buf()
                rstd = sbt(es, "a1_rstd", [128, TS], F32)
                rsb = Buf()
                hts = [sbt(es, "a1_h%d" % i, [128, KC, TS], BF16) for i in range(2)]
                hbs = [Buf() for _ in range(2)]
                ssum = pst(es, "a1_ss", [128, TS])
                ssb = Buf()
                banks = [pst(es, "a1_b%d" % i, [128, TS]) for i in range(4)]
                bbs = [Buf() for _ in range(4)]
                ub = Buf("uT")
                for i in range(NT):
                    xt, xb = xts[i % 2], xbs[i % 2]
                    ht, hb = hts[i % 2], hbs[i % 2]
                    P.dma("sp", xt[:], xtile(xin, i), writes=[xb])
                    rms_rstd(P, xt, xb, sq[:], sqb, ssum[:], ssb, rstd[:], rsb)
                    rms_apply(P, xt, xb, rstd[:], rsb, 0, ht, hb)
                    for fo in range(KC):
                        bk, bb = banks[fo % 4], bbs[fo % 4]

                        def mm(e, fo=fo, bk=bk, ht=ht):
                            r = None
                            for kc in range(KC):
                                r = e.matmul(bk[:], lhsT=w_sb[:, kc, fo * 128:(fo + 1) * 128], rhs=ht[:, kc, :], start=(kc == 0), stop=(kc == KC - 1))
                            return r

                        P.op("pe", mm, [wb, hb], [bb])
                        dst = uT[:, fo, i * TS:(i + 1) * TS]
                        if fo % 2 == 0:
                            P.op("act", lambda e, bk=bk, dst=dst: e.activation(out=dst, in_=bk[:], func=AF.Copy), [bb], [ub])
                        else:
                            P.op("dve", lambda e, bk=bk, dst=dst: e.tensor_copy(out=dst, in_=bk[:]), [bb], [ub])
                if UD is not None:
                    P.dma("sp", UD.ap().rearrange("c p t -> p c t"), uT[:], reads=[ub], writes=[Buf()])
                P.emit()

        def phase_A2(uT):
            with contextlib.ExitStack() as es:
                P = Prog(G)
                are = sbt(es, "s_are", [128, NPAIR], F32)
                aim = sbt(es, "s_aim", [128, NPAIR], F32)
                ldt = sbt(es, "s_ldt", [128, NPAIR], F32)
                bre = sbt(es, "s_bre", [128, NPAIR, 16], F32)
                bim = sbt(es, "s_bim", [128, NPAIR, 16], F32)
                cre = sbt(es, "s_cre", [128, NPAIR, 16], F32)
                cim = sbt(es, "s_cim", [128, NPAIR, 16], F32)
                dsk = sbt(es, "s_dsk", [128, KC], F32)
                pb = Buf("params")
                for t, s in ((are, s_are), (aim, s_aim), (ldt, s_ldt), (bre, s_bre), (bim, s_bim), (cre, s_cre), (cim, s_cim), (dsk, s_d)):
                    P.dma("sp", t[:], s.ap(), writes=[pb])
                names = ["dt", "xr", "r", "th", "fr", "sn", "hf", "cs", "abre", "abim", "den", "m1", "t1", "t2", "fre", "fim"]
                T = {n: sbt(es, "s_" + n, [128, NPAIR], F32) for n in names}
                ki = sbt(es, "s_ki", [128, NPAIR], I32)
                tb = Buf("tiny")

                def tiny(eng, fn):
                    P.op(eng, fn, [pb, tb], [tb])

                tiny("act", lambda e: e.activation(out=T["dt"][:], in_=ldt[:], func=AF.Exp))
                tiny("dve", lambda e: e.tensor_tensor(out=T["xr"][:], in0=are[:], in1=T["dt"][:], op=ALU.mult))
                tiny("act", lambda e: e.activation(out=T["r"][:], in_=T["xr"][:], func=AF.Exp))
                tiny("dve", lambda e: e.tensor_tensor(out=T["th"][:], in0=aim[:], in1=T["dt"][:], op=ALU.mult))
                tiny("dve", lambda e: e.tensor_scalar(out=T["fr"][:], in0=T["th"][:], scalar1=1.0 / (2.0 * math.pi), scalar2=None, op0=ALU.mult))
                tiny("dve", lambda e: e.tensor_copy(out=ki[:], in_=T["fr"][:]))
                tiny("dve", lambda e: e.tensor_tensor(out=T["fr"][:], in0=T["fr"][:], in1=ki[:], op=ALU.subtract))
                tiny("act", lambda e: e.activation(out=T["sn"][:], in_=T["fr"][:], func=AF.Sin, scale=TWO_PI_LO))
                tiny("act", lambda e: e.activation(out=T["hf"][:], in_=T["fr"][:], func=AF.Sin, scale=PI_LO))
                tiny("dve", lambda e: e.tensor_tensor(out=T["cs"][:], in0=T["hf"][:], in1=T["hf"][:], op=ALU.mult))
                tiny("dve", lambda e: e.tensor_scalar(out=T["cs"][:], in0=T["cs"][:], scalar1=-2.0, scalar2=1.0, op0=ALU.mult, op1=ALU.add))
                tiny("dve", lambda e: e.tensor_tensor(out=T["abre"][:], in0=T["r"][:], in1=T["cs"][:], op=ALU.mult))
                tiny("dve", lambda e: e.tensor_tensor(out=T["abim"][:], in0=T["r"][:], in1=T["sn"][:], op=ALU.mult))
                tiny("dve", lambda e: e.tensor_tensor(out=T["den"][:], in0=are[:], in1=are[:], op=ALU.mult))
                tiny("dve", lambda e: e.tensor_tensor(out=T["t1"][:], in0=aim[:], in1=aim[:], op=ALU.mult))
                tiny("dve", lambda e: e.tensor_tensor(out=T["den"][:], in0=T["den"][:], in1=T["t1"][:], op=ALU.add))
                tiny("dve", lambda e: e.reciprocal(out=T["den"][:], in_=T["den"][:]))
                tiny("dve", lambda e: e.tensor_scalar(out=T["m1"][:], in0=T["abre"][:], scalar1=-1.0, scalar2=None, op0=ALU.add))
                tiny("dve", lambda e: e.tensor_tensor(out=T["t1"][:], in0=T["m1"][:], in1=are[:], op=ALU.mult))
                tiny("dve", lambda e: e.tensor_tensor(out=T["t2"][:], in0=T["abim"][:], in1=aim[:], op=ALU.mult))
                tiny("dve", lambda e: e.tensor_tensor(out=T["t1"][:], in0=T["t1"][:], in1=T["t2"][:], op=ALU.add))
                tiny("dve", lambda e: e.tensor_tensor(out=T["fre"][:], in0=T["t1"][:], in1=T["den"][:], op=ALU.mult))
                tiny("dve", lambda e: e.tensor_tensor(out=T["t1"][:], in0=T["abim"][:], in1=are[:], op=ALU.mult))
                tiny("dve", lambda e: e.tensor_tensor(out=T["t2"][:], in0=T["m1"][:], in1=aim[:], op=ALU.mult))
                tiny("dve", lambda e: e.tensor_tensor(out=T["t1"][:], in0=T["t1"][:], in1=T["t2"][:], op=ALU.subtract))
                tiny("dve", lambda e: e.tensor_tensor(out=T["fim"][:], in0=T["t1"][:], in1=T["den"][:], op=ALU.mult))

                BbT = sbt(es, "s_BbT", [128, NPAIR, 2, 128], BF16)
                CTr = sbt(es, "s_CTr", [128, NPAIR, 128], BF16)
                CTi = sbt(es, "s_CTi", [128, NPAIR, 128], BF16)
                ident = sbt(es, "s_ident", [128, 128], F32)
                carry = sbt(es, "s_carry", [128, NPAIR, 2], F32)
                iota_i = sbt(es, "s_iotai", [128, TS], I32)
                iota_f = sbt(es, "s_iotaf", [128, TS], F32)
                cbuf = Buf("consts")
                matb = Buf("mats")

                def mk_ident(e):
                    e.memset(ident[:], 1.0)
                    return e.affine_select(out=ident[:], in_=ident[:], pattern=[[-1, 128]], compare_op=ALU.is_equal, fill=0.0, base=0, channel_multiplier=1)

                P.op("pool", mk_ident, [], [cbuf])
                P.op("pool", lambda e: e.iota(iota_i[:], [[1, TS]], base=1, channel_multiplier=0), [], [cbuf])
                P.op("pool", lambda e: e.tensor_copy(out=iota_f[:], in_=iota_i[:]), [cbuf], [cbuf])
                P.op("pool", lambda e: e.memset(carry[:], 0.0), [], [cbuf])
                P.op("pool", lambda e: e.memset(CTr[:], 0.0), [], [matb])
                P.op("pool", lambda e: e.memset(CTi[:], 0.0), [], [matb])

                with contextlib.ExitStack() as es2:
                    padr = sbt(es2, "s_padr", [128, NPAIR, 128], F32)
                    padi = sbt(es2, "s_padi", [128, NPAIR, 128], F32)
                    tmp1 = sbt(es2, "s_tmp1", [128, NPAIR, 16], F32)
                    tmp2 = sbt(es2, "s_tmp2", [128, NPAIR, 16], F32)
                    padb = Buf("pad")
                    P.op("pool", lambda e: e.memset(padr[:], 0.0), [], [padb])
                    P.op("pool", lambda e: e.memset(padi[:], 0.0), [], [padb])

                    def padview(t, gl, dtsz=None):
                        return bass.AP(t, 64 * gl * NPAIR * 128 + 16 * gl, [[NPAIR * 128, 64], [512, 8], [160, 4], [1, 16]])

                    def bview(t, gl):
                        return bass.AP(t, 64 * gl * NPAIR * 16, [[NPAIR * 16, 64], [64, 8], [16, 4], [1, 16]])

                    def fview(t, gl):
                        return bass.AP(t, 64 * gl * NPAIR, [[NPAIR, 64], [4, 8], [1, 4], [0, 16]])

                    for gl in range(2):
                        def bb(e, gl=gl):
                            t1v, t2v = bview(tmp1, gl), bview(tmp2, gl)
                            e.tensor_tensor(out=t1v, in0=bview(bre, gl), in1=fview(T["fre"], gl), op=ALU.mult)
                            e.tensor_tensor(out=t2v, in0=bview(bim, gl), in1=fview(T["fim"], gl), op=ALU.mult)
                            return e.tensor_tensor(out=padview(padr, gl), in0=t1v, in1=t2v, op=ALU.subtract)

                        P.op("pool", bb, [pb, tb, padb], [padb])

                        def bb2(e, gl=gl):
                            t1v, t2v = bview(tmp1, gl), bview(tmp2, gl)
                            e.tensor_tensor(out=t1v, in0=bview(bim, gl), in1=fview(T["fre"], gl), op=ALU.mult)
                            e.tensor_tensor(out=t2v, in0=bview(bre, gl), in1=fview(T["fim"], gl), op=ALU.mult)
                            return e.tensor_tensor(out=padview(padi, gl), in0=t1v, in1=t2v, op=ALU.add)

                        P.op("pool", bb2, [pb, tb, padb], [padb])
                        P.op("dve", lambda e, gl=gl: e.tensor_copy(out=padview(CTr, gl), in_=bview(cre, gl)), [pb, matb], [matb])
                        P.op("dve", lambda e, gl=gl: e.tensor_scalar(out=padview(CTi, gl), in0=bview(cim, gl), scalar1=-1.0, scalar2=None, op0=ALU.mult), [pb, matb], [matb])

                    trb = [pst(es2, "s_trb%d" % i, [128, TS]) for i in range(2)]
                    trbb = [Buf() for _ in range(2)]
                    n = 0
                    for j0 in range(0, NPAIR, 2):
                        bk, bb_ = trb[n % 2], trbb[n % 2]
                        n += 1

                        def trs(e, j0=j0, bk=bk):
                            r = None
                            for jj in range(2):
                                for ri, pad in enumerate((padr, padi)):
                                    k = jj * 2 + ri
                                    r = e.transpose(out=bk[:, k * 128:(k + 1) * 128], in_=pad[:, j0 + jj, :], identity=ident[:])
                            return r

                        P.op("pe", trs, [padb, cbuf], [bb_])
                        dst = BbT[:, j0:j0 + 2, :, :].rearrange("p a b c -> p (a b c)")
                        if n % 2 == 0:
                            P.op("act", lambda e, bk=bk, dst=dst: e.activation(out=dst, in_=bk[:], func=AF.Copy), [bb_], [matb])
                        else:
                            P.op("dve", lambda e, bk=bk, dst=dst: e.tensor_copy(out=dst, in_=bk[:]), [bb_], [matb])
                    if dump is not None and "DBG" in dump:
                        def dd(name, t, dt):
                            shp = list(t.shape)
                            o = nc.dram_tensor("DBG_" + name, shp, dt, kind="ExternalOutput")
                            P.dma("sp", o.ap(), t[:], reads=[pb, tb, padb, matb, cbuf], writes=[Buf()])
                        for nme in ("r", "fr", "fre", "fim", "cs", "sn"):
                            dd(nme, T[nme], F32)
                        dd("padr", padr, F32)
                        dd("padi", padi, F32)
                        dd("CTr", CTr, BF16)
                        dd("CTi", CTi, BF16)
                        dd("BbT", BbT, BF16)
                    P.emit()
                if stop_after < 2:
                    return
                P = Prog(G)

                Ec = sbt(es, "s_Ec", [128, 4, TS], F32)
                Es = sbt(es, "s_Es", [128, 4, TS], F32)
                tabb = [Buf() for _ in range(4)]
                u1 = sbt(es, "s_u1", [128, TS], F32)
                u1i = sbt(es, "s_u1i", [128, TS], I32)
                hfb = sbt(es, "s_hfb", [128, TS], F32)
                u1b = Buf()
                NW = 2
                mtiles = [[sbt(es, "s_m%d_%d" % (k, w), [128, TS], F32) for k in range(4)] for w in range(NW)]
                mb = [[Buf() for k in range(4)] for w in range(NW)]
                btr = [sbt(es, "s_btr%d" % w, [128, TS], F32) for w in range(NW)]
                bti = [sbt(es, "s_bti%d" % w, [128, TS], F32) for w in range(NW)]
                btb = [[Buf(), Buf()] for w in range(NW)]
                str_ = [sbt(es, "s_str%d" % w, [128, TS], F32) for w in range(NW)]
                sti = [sbt(es, "s_sti%d" % w, [128, TS], F32) for w in range(NW)]
                stb = [[Buf(), Buf()] for w in range(NW)]
                NN = 3
                ntl = [[sbt(es, "s_n%d_%d" % (k, w), [128, TS], BF16) for k in range(4)] for w in range(NN)]
                nb = [[Buf() for k in range(4)] for w in range(NN)]
                ctmp = sbt(es, "s_ctmp", [128, 2], F32)
                ctb = Buf()
                carb = [Buf() for _ in range(NPAIR)]
                yv = [sbt(es, "s_yv%d" % w, [128, TS], F32) for w in range(2)]
                yt = [sbt(es, "s_yt%d" % w, [128, TS], F32) for w in range(2)]
                yb = [Buf() for _ in range(2)]
                pbu = [[pst(es, "s_pbu%d_%d" % (ri, w), [128, TS]) for ri in range(2)] for w in range(2)]
                pbub = [[Buf(), Buf()] for w in range(2)]
                pY = [pst(es, "s_pY%d" % w, [128, TS]) for w in range(2)]
                pYb = [Buf() for _ in range(2)]
                ubs = [[Buf() for _ in range(NT)] for _ in range(KC)]
                steps = [(q, i, jj) for q in range(KC) for i in range(NT) for jj in range(4)]

                def emit_bu(idx):
                    q, i, jj = steps[idx]
                    j = 4 * q + jj
                    w2 = idx % 2
                    tsl = slice(i * TS, (i + 1) * TS)
                    for ri in range(2):
                        P.op("pe", lambda e, ri=ri, j=j, w2=w2, q=q, tsl=tsl: e.matmul(pbu[w2][ri][:], lhsT=BbT[:, j, ri, :], rhs=uT[:, q, tsl], start=True, stop=True), [matb, ubs[q][i]], [pbub[w2][ri]])

                emit_bu(0)
                for idx, (q, i, jj) in enumerate(steps):
                    if idx + 1 < len(steps):
                        emit_bu(idx + 1)
                    if i == 0 and jj == 0:
                        for j2 in range(4):
                            j = 4 * q + j2
                            P.op("dve", lambda e, j=j: e.tensor_scalar(out=u1[:], in0=iota_f[:], scalar1=T["fr"][:, j:j + 1], scalar2=None, op0=ALU.mult), [cbuf, tb, u1b], [u1b])
                            P.op("dve", lambda e: e.tensor_copy(out=u1i[:], in_=u1[:]), [u1b], [u1b])
                            P.op("dve", lambda e: e.tensor_tensor(out=u1[:], in0=u1[:], in1=u1i[:], op=ALU.subtract), [u1b], [u1b])
                            P.op("act", lambda e, j2=j2: e.activation(out=Es[:, j2, :], in_=u1[:], func=AF.Sin, scale=TWO_PI_LO), [u1b], [tabb[j2]])
                            P.op("act", lambda e: e.activation(out=hfb[:], in_=u1[:], func=AF.Sin, scale=PI_LO), [u1b], [u1b])
                            P.op("act", lambda e: e.activation(out=hfb[:], in_=hfb[:], func=AF.Square), [u1b], [u1b])
                            P.op("act", lambda e, j2=j2: e.activation(out=Ec[:, j2, :], in_=hfb[:], func=AF.Identity, scale=-2.0, bias=1.0), [u1b], [tabb[j2], u1b])
                    j = 4 * q + jj
                    tsl = slice(i * TS, (i + 1) * TS)
                    yi = (q * NT + i) % 2
                    w = idx % NW
                    w2 = idx % 2
                    w3 = idx % NN
                    ec, es_ = Ec[:, jj, :], Es[:, jj, :]
                    m = mtiles[w]
                    def g1(e, m=m, w2=w2, ec=ec, es_=es_):
                        e.tensor_tensor(out=m[0][:], in0=pbu[w2][0][:], in1=ec, op=ALU.mult)
                        e.tensor_tensor(out=m[1][:], in0=pbu[w2][1][:], in1=es_, op=ALU.mult)
                        e.tensor_tensor(out=m[2][:], in0=pbu[w2][1][:], in1=ec, op=ALU.mult)
                        return e.tensor_tensor(out=m[3][:], in0=pbu[w2][0][:], in1=es_, op=ALU.mult)

                    P.op("dve", g1, [pbub[w2][0], pbub[w2][1], tabb[jj]], mb[w])

                    def g2(e, m=m, w=w):
                        e.tensor_tensor(out=btr[w][:], in0=m[0][:], in1=m[1][:], op=ALU.add)
                        return e.tensor_tensor(out=bti[w][:], in0=m[2][:], in1=m[3][:], op=ALU.subtract)

                    P.op("dve", g2, mb[w], btb[w])
                    rb = T["r"][:, j:j + 1].to_broadcast([128, TS])

                    def g3(e, w=w, j=j, rb=rb):
                        e.tensor_tensor_scan(out=str_[w][:], data0=rb, data1=btr[w][:], initial=carry[:, j, 0:1], op0=ALU.mult, op1=ALU.add)
                        return e.tensor_tensor_scan(out=sti[w][:], data0=rb, data1=bti[w][:], initial=carry[:, j, 1:2], op0=ALU.mult, op1=ALU.add)

                    P.op("dve", g3, btb[w] + [carb[j], tb], stb[w])
                    nn = ntl[w3]
                    L1 = slice(TS - 1, TS)

                    def g4(e, nn=nn, w=w, ec=ec, es_=es_, jj=jj):
                        e.tensor_tensor(out=nn[0][:], in0=str_[w][:], in1=ec, op=ALU.mult)
                        e.scalar_tensor_tensor(out=nn[1][:], in0=sti[w][:], scalar=-1.0, in1=es_, op0=ALU.mult, op1=ALU.mult)
                        e.tensor_tensor(out=nn[2][:], in0=str_[w][:], in1=es_, op=ALU.mult)
                        e.tensor_tensor(out=nn[3][:], in0=sti[w][:], in1=ec, op=ALU.mult)
                        e.tensor_tensor(out=ctmp[:, 0:1], in0=sti[w][:, L1], in1=Es[:, jj, L1], op=ALU.mult)
                        return e.tensor_tensor(out=ctmp[:, 1:2], in0=sti[w][:, L1], in1=Ec[:, jj, L1], op=ALU.mult)

                    P.op("dve", g4, stb[w] + [tabb[jj], ctb], nb[w3] + [ctb])

                    def g5(e, w=w, jj=jj, j=j):
                        e.scalar_tensor_tensor(out=carry[:, j, 0:1], in0=str_[w][:, L1], scalar=Ec[:, jj, L1], in1=ctmp[:, 0:1], op0=ALU.mult, op1=ALU.subtract)
                        return e.scalar_tensor_tensor(out=carry[:, j, 1:2], in0=str_[w][:, L1], scalar=Es[:, jj, L1], in1=ctmp[:, 1:2], op0=ALU.mult, op1=ALU.add)

                    P.op("dve", g5, [stb[w][0], tabb[jj], ctb, carb[j]], [carb[j], ctb])

                    def ymm(e, j=j, nn=nn, jj=jj, yi=yi):
                        e.matmul(pY[yi][:], lhsT=CTr[:, j, :], rhs=nn[0][:], start=(jj == 0), stop=False)
                        e.matmul(pY[yi][:], lhsT=CTr[:, j, :], rhs=nn[1][:], start=False, stop=False)
                        e.matmul(pY[yi][:], lhsT=CTi[:, j, :], rhs=nn[2][:], start=False, stop=False)
                        return e.matmul(pY[yi][:], lhsT=CTi[:, j, :], rhs=nn[3][:], start=False, stop=(jj == 3))

                    P.op("pe", ymm, [matb] + nb[w3], [pYb[yi]])
                    if jj == 3:
                        uv = uT[:, q, tsl]
                        P.op("dve", lambda e, yi=yi, uv=uv, q=q: e.scalar_tensor_tensor(out=yv[yi][:], in0=uv, scalar=dsk[:, q:q + 1], in1=pY[yi][:], op0=ALU.mult, op1=ALU.add), [pYb[yi], ubs[q][i], pb], [yb[yi]])
                        P.op("act", lambda e, yi=yi: e.activation(out=yt[yi][:], in_=yv[yi][:], func=AF.Square), [yb[yi]], [yb[yi]])
                        P.op("act", lambda e, yi=yi: e.activation(out=yt[yi][:], in_=yt[yi][:], func=AF.Identity, scale=0.044715, bias=1.0), [yb[yi]], [yb[yi]])
                        P.op("dve", lambda e, yi=yi: e.tensor_tensor(out=yt[yi][:], in0=yt[yi][:], in1=yv[yi][:], op=ALU.mult), [yb[yi]], [yb[yi]])
                        P.op("act", lambda e, yi=yi: e.activation(out=yt[yi][:], in_=yt[yi][:], func=AF.Sigmoid, scale=2.0 * math.sqrt(2.0 / math.pi)), [yb[yi]], [yb[yi]])
                        P.op("dve", lambda e, yi=yi, uv=uv: e.tensor_tensor(out=uv, in0=yt[yi][:], in1=yv[yi][:], op=ALU.mult), [yb[yi]], [ubs[q][i]])
                if UD is not None:
                    P.dma("sp", UD.ap().rearrange("c p t -> p c t"), uT[:], reads=[b for row in ubs for b in row], writes=[Buf()])
                P.emit()

        def phase_A3(uT):
            with contextlib.ExitStack() as es:
                P = Prog(G)
                w_sb = sbt(es, "a3_w", [128, KC, 2 * D], BF16)
                wb = Buf("w")
                load_w_cast(P, lambda rc, c0, c1: w_sb[:, rc, c0:c1], w_glu.ap(), KC, 2 * D, wb)
                xts = [sbt(es, "a3_x%d" % i, [128, KC, TS], F32) for i in range(2)]
                xbs = [[Buf() for _ in range(KC)] for _ in range(2)]
                sg = [sbt(es, "a3_sg%d" % i, [128, TS], F32) for i in range(2)]
                sgb = [Buf() for _ in range(2)]
                pa = [pst(es, "a3_pa%d" % i, [128, TS]) for i in range(2)]
                pab = [Buf() for _ in range(2)]
                pbk = [pst(es, "a3_pb%d" % i, [128, TS]) for i in range(2)]
                pbb = [Buf() for _ in range(2)]
                n = 0
                for i in range(NT):
                    xt, xb = xts[i % 2], xbs[i % 2]
                    tsl = slice(i * TS, (i + 1) * TS)
                    P.dma("sp", xt[:], xtile(xin, i), writes=xb)
                    for fo in range(KC):
                        k = n % 2
                        n += 1

                        def mma(e, fo=fo, k=k, c0=0, bank=pa, tsl=tsl):
                            r = None
                            for kc in range(KC):
                                r = e.matmul(bank[k][:], lhsT=w_sb[:, kc, c0 + fo * 128:c0 + (fo + 1) * 128], rhs=uT[:, kc, tsl], start=(kc == 0), stop=(kc == KC - 1))
                            return r

                        P.op("pe", mma, [wb], [pab[k]])
                        P.op("pe", lambda e, fo=fo, k=k, mma=mma: mma(e, fo, k, D, pbk), [wb], [pbb[k]])
                        P.op("act", lambda e, k=k: e.activation(out=sg[k][:], in_=pbk[k][:], func=AF.Sigmoid), [pbb[k]], [sgb[k]])
                        P.op("dve", lambda e, k=k: e.tensor_tensor(out=sg[k][:], in0=pa[k][:], in1=sg[k][:], op=ALU.mult), [pab[k], sgb[k]], [sgb[k]])
                        P.op("pool", lambda e, k=k, fo=fo, xt=xt: e.tensor_tensor(out=xt[:, fo, :], in0=xt[:, fo, :], in1=sg[k][:], op=ALU.add), [sgb[k], xb[fo]], [xb[fo]])
                    P.dma("sp", xtile(XT, i), xt[:], reads=xb, writes=[Buf()])
                P.emit()

        def phase_FFN(l, last):
            with contextlib.ExitStack() as es:
                P = Prog(G)
                wup = sbt(es, "f_wup", [128, KC, 2 * DFF], BF16)
                wdn = sbt(es, "f_wdn", [128, FC, D], BF16)
                wub, wdb = Buf("wup"), Buf("wdn")
                load_w_cast(P, lambda rc, c0, c1: wup[:, rc, c0:c1], w_up.ap()[l], KC, 2 * DFF, wub)
                load_w_cast(P, lambda rc, c0, c1: wdn[:, rc, c0:c1], w_dn.ap()[l], FC, D, wdb)
                cw_sb = sbt(es, "f_cw", [128, FC, 3], F32)
                cb_sb = sbt(es, "f_cb", [128, FC], F32)
                cpb = Buf()
                P.dma("sp", cw_sb[:], cw.ap()[:, l], writes=[cpb])
                P.dma("sp", cb_sb[:], cb.ap()[:, l], writes=[cpb])
                xt = sbt(es, "f_x", [128, KC, TS], F32)
                xb = [Buf() for _ in range(KC)]
                aT = sbt(es, "f_aT", [128, FC, TS], BF16)
                ab = [Buf() for _ in range(FC)]
                sq = aT[:, 0:KC, :].rearrange("p c t -> p (c t)")
                rstd = sbt(es, "f_rstd", [128, TS], F32)
                rsb = Buf()
                ht = sbt(es, "f_h", [128, KC, TS], BF16)
                hb = Buf()
                halo = sbt(es, "f_halo", [128, FC, 2], F32)
                hlb = [Buf() for _ in range(FC)]
                NG = 3
                gbuf = [sbt(es, "f_g%d" % i, [128, TS + 2], F32) for i in range(NG)]
                gbb = [Buf() for _ in range(NG)]
                a0 = [sbt(es, "f_a%d" % i, [128, TS], F32) for i in range(NG)]
                a0b = [Buf() for _ in range(NG)]
                ssum = pst(es, "f_ss", [128, TS])
                ssb = Buf()
                pg = [pst(es, "f_pg%d" % i, [128, TS]) for i in range(2)]
                pgb = [Buf() for _ in range(2)]
                pu = [pst(es, "f_pu%d" % i, [128, TS]) for i in range(2)]
                pub = [Buf() for _ in range(2)]
                pd = [pst(es, "f_pd%d" % i, [128, TS]) for i in range(2)]
                pdb = [Buf() for _ in range(2)]
                P.op("pool", lambda e: e.memset(halo[:], 0.0), [], hlb)
                n = 0
                for i in range(NT):
                    P.dma("sp", xt[:], xtile(XT, i), writes=xb)
                    P.op("act", lambda e: e.activation(out=sq, in_=xt[:].rearrange("p c t -> p (c t)"), func=AF.Square), xb, ab[0:KC])

                    def mmss(e):
                        r = None
                        for kc in range(KC):
                            r = e.matmul(ssum[:], lhsT=ones_bf[:], rhs=aT[:, kc, :], start=(kc == 0), stop=(kc == KC - 1))
                        return r

                    P.op("pe", mmss, ab[0:KC], [ssb])
                    P.op("act", lambda e: e.activation(out=rstd[:], in_=ssum[:], func=AF.Sqrt, scale=1.0 / D, bias=1e-6), [ssb], [rsb])
                    P.op("dve", lambda e: e.reciprocal(out=rstd[:], in_=rstd[:]), [rsb], [rsb])
                    for kc in range(KC):
                        P.op("dve", lambda e, kc=kc: e.scalar_tensor_tensor(out=ht[:, kc, :], in0=xt[:, kc, :], scalar=gam_sb[:, 1 + 3 * l, kc:kc + 1], in1=rstd[:], op0=ALU.mult, op1=ALU.mult), [xb[kc], rsb], [hb])
                    for fc in range(FC):
                        k = n % 2
                        g3 = n % NG
                        n += 1

                        def mmg(e, fc=fc, k=k, c0=0, bank=pg):
                            r = None
                            for kc in range(KC):
                                r = e.matmul(bank[k][:], lhsT=wup[:, kc, c0 + fc * 128:c0 + (fc + 1) * 128], rhs=ht[:, kc, :], start=(kc == 0), stop=(kc == KC - 1))
                            return r

                        P.op("pe", mmg, [wub, hb], [pgb[k]])
                        P.op("pe", lambda e, fc=fc, k=k: mmg(e, fc, k, DFF, pu), [wub, hb], [pub[k]])
                        gb = gbuf[g3]
                        P.op("pool", lambda e, gb=gb, fc=fc: e.tensor_copy(out=gb[:, 0:2], in_=halo[:, fc, :]), [hlb[fc]], [gbb[g3]])
                        P.op("act", lambda e, gb=gb, k=k: e.activation(out=gb[:, 2:TS + 2], in_=pg[k][:], func=AF.Copy), [pgb[k], gbb[g3]], [gbb[g3]])
                        P.op("pool", lambda e, gb=gb, fc=fc: e.tensor_copy(out=halo[:, fc, :], in_=gb[:, TS:TS + 2]), [gbb[g3]], [hlb[fc]])
                        P.op("act", lambda e, k=k, g3=g3, fc=fc: e.activation(out=a0[g3][:], in_=pg[k][:], func=AF.Identity, scale=cw_sb[:, fc, 2:3], bias=cb_sb[:, fc:fc + 1]), [pgb[k], cpb], [a0b[g3]])
                        P.op("dve", lambda e, gb=gb, g3=g3, fc=fc: e.scalar_tensor_tensor(out=a0[g3][:], in0=gb[:, 1:TS + 1], scalar=cw_sb[:, fc, 1:2], in1=a0[g3][:], op0=ALU.mult, op1=ALU.add), [gbb[g3], a0b[g3], cpb], [a0b[g3]])
                        P.op("dve", lambda e, gb=gb, g3=g3, fc=fc: e.scalar_tensor_tensor(out=a0[g3][:], in0=gb[:, 0:TS], scalar=cw_sb[:, fc, 0:1], in1=a0[g3][:], op0=ALU.mult, op1=ALU.add), [gbb[g3], a0b[g3], cpb], [a0b[g3]])
                        P.op("act", lambda e, g3=g3: e.activation(out=a0[g3][:], in_=a0[g3][:], func=AF.Silu), [a0b[g3]], [a0b[g3]])
                        P.op("dve", lambda e, g3=g3, k=k, fc=fc: e.tensor_tensor(out=aT[:, fc, :], in0=a0[g3][:], in1=pu[k][:], op=ALU.mult), [a0b[g3], pub[k], ssb], [ab[fc]])
                    for fo in range(KC):
                        k = fo % 2

                        def mmd(e, fo=fo, k=k):
                            r = None
                            for fc in range(FC):
                                r = e.matmul(pd[k][:], lhsT=wdn[:, fc, fo * 128:(fo + 1) * 128], rhs=aT[:, fc, :], start=(fc == 0), stop=(fc == FC - 1))
                            return r

                        P.op("pe", mmd, [wdb] + ab, [pdb[k]])
                        P.op("dve", lambda e, fo=fo, k=k: e.tensor_tensor(out=xt[:, fo, :], in0=pd[k][:], in1=xt[:, fo, :], op=ALU.add), [pdb[k], xb[fo]], [xb[fo]])
                    if not last:
                        P.dma("sp", xtile(XT, i), xt[:], reads=xb, writes=[Buf()])
                    else:
                        P.op("act", lambda e: e.activation(out=sq, in_=xt[:].rearrange("p c t -> p (c t)"), func=AF.Square), xb, ab[0:KC])
                        P.op("pe", mmss, ab[0:KC], [ssb])
                        P.op("act", lambda e: e.activation(out=rstd[:], in_=ssum[:], func=AF.Sqrt, scale=1.0 / D, bias=1e-6), [ssb], [rsb])
                        P.op("dve", lambda e: e.reciprocal(out=rstd[:], in_=rstd[:]), [rsb], [rsb])
                        for kc in range(KC):
                            P.op("dve", lambda e, kc=kc: e.scalar_tensor_tensor(out=xt[:, kc, :], in0=xt[:, kc, :], scalar=gam_sb[:, 5, kc:kc + 1], in1=rstd[:], op0=ALU.mult, op1=ALU.mult), [xb[kc], rsb], [xb[kc]])
                        P.dma("sp", xtile(outT, i), xt[:], reads=xb, writes=[Buf()])
                P.emit()

        def phase_CD():
            with contextlib.ExitStack() as es:
                P = Prog(G)
                kvw = sbt(es, "c_kvw", [128, KC, 2 * D], BF16)
                wq = sbt(es, "c_wq", [128, KC, D], BF16)
                kvb, wqb = Buf(), Buf()
                load_w_cast(P, lambda rc, c0, c1: kvw[:, rc, c0:c1], kv_w.ap(), KC, 2 * D, kvb)
                load_w_cast(P, lambda rc, c0, c1: wq[:, rc, c0:c1], w_q.ap(), KC, D, wqb)
                xts = [sbt(es, "c_x%d" % i, [128, KC, TS], F32) for i in range(2)]
                xbs = [Buf() for _ in range(2)]
                sq = sbt(es, "c_sq", [128, KC * TS], BF16)
                sqb = Buf()
                rstd = sbt(es, "c_rstd", [128, TS], F32)
                rsb = Buf()
                hk = sbt(es, "c_hk", [128, KC, TS], BF16)
                hq = sbt(es, "c_hq", [128, KC, TS], BF16)
                hkb, hqb = Buf(), Buf()
                kt = [sbt(es, "c_kt%d" % i, [128, KC, TS], BF16) for i in range(2)]
                qt = [sbt(es, "c_qt%d" % i, [128, KC, TS], BF16) for i in range(2)]
                vt = [sbt(es, "c_vt%d" % i, [128, 4, D], BF16) for i in range(2)]
                ktb = [Buf() for _ in range(2)]
                qtb = [Buf() for _ in range(2)]
                vtb = [Buf() for _ in range(2)]
                ssum = pst(es, "c_ss", [128, TS])
                ssb = Buf()
                banks = [pst(es, "c_b%d" % i, [128, TS]) for i in range(6)]
                bbs = [Buf() for _ in range(6)]
                n = 0

                def evac(bk, bb, dst, dstb, scale=None):
                    nonlocal n
                    n += 1
                    if n % 2 == 0:
                        if scale is None:
                            P.op("act", lambda e: e.activation(out=dst, in_=bk[:], func=AF.Copy), [bb], [dstb])
                        else:
                            P.op("act", lambda e: e.activation(out=dst, in_=bk[:], func=AF.Copy, scale=scale), [bb], [dstb])
                    else:
                        if scale is None:
                            P.op("dve", lambda e: e.tensor_copy(out=dst, in_=bk[:]), [bb], [dstb])
                        else:
                            P.op("dve", lambda e: e.tensor_scalar(out=dst, in0=bk[:], scalar1=scale, scalar2=None, op0=ALU.mult), [bb], [dstb])

                nb = 0
                for i in range(NT):
                    xt, xb = xts[i % 2], xbs[i % 2]
                    s = i % 2
                    P.dma("sp", xt[:], xtile(XT, i), writes=[xb])
                    rms_rstd(P, xt, xb, sq[:], sqb, ssum[:], ssb, rstd[:], rsb)
                    rms_apply(P, xt, xb, rstd[:], rsb, 2, hk, hkb)
                    rms_apply(P, xt, xb, rstd[:], rsb, 3, hq, hqb)
                    for fo in range(KC):
                        bk, bb = banks[nb % 6], bbs[nb % 6]
                        nb += 1

                        def mmk(e, fo=fo, bk=bk):
                            r = None
                            for kc in range(KC):
                                r = e.matmul(bk[:], lhsT=kvw[:, kc, fo * 128:(fo + 1) * 128], rhs=hk[:, kc, :], start=(kc == 0), stop=(kc == KC - 1))
                            return r

                        P.op("pe", mmk, [kvb, hkb], [bb])
                        evac(bk, bb, kt[s][:, fo, :], ktb[s])
                    for fo in range(KC):
                        bk, bb = banks[nb % 6], bbs[nb % 6]
                        nb += 1

                        def mmq(e, fo=fo, bk=bk):
                            r = None
                            for kc in range(KC):
                                r = e.matmul(bk[:], lhsT=wq[:, kc, fo * 128:(fo + 1) * 128], rhs=hq[:, kc, :], start=(kc == 0), stop=(kc == KC - 1))
                            return r

                        P.op("pe", mmq, [wqb, hqb], [bb])
                        evac(bk, bb, qt[s][:, fo, :], qtb[s], scale=0.125)
                    for blk in range(4):
                        for half in range(2):
                            bk, bb = banks[nb % 6], bbs[nb % 6]
                            nb += 1

                            def mmv(e, blk=blk, half=half, bk=bk):
                                r = None
                                for kc in range(KC):
                                    r = e.matmul(bk[:], lhsT=hk[:, kc, blk * 128:(blk + 1) * 128], rhs=kvw[:, kc, D + half * 512:D + (half + 1) * 512], start=(kc == 0), stop=(kc == KC - 1))
                                return r

                            P.op("pe", mmv, [kvb, hkb], [bb])
                            evac(bk, bb, vt[s][:, blk, half * 512:(half + 1) * 512], vtb[s])
                    P.dma("sp", xtile(KT, i), kt[s][:], reads=[ktb[s]], writes=[Buf()])
                    P.dma("sp", xtile(QT, i), qt[s][:], reads=[qtb[s]], writes=[Buf()])
                    P.dma("sp", VS.ap()[i * 4:(i + 1) * 4].rearrange("b p d -> p b d"), vt[s][:], reads=[vtb[s]], writes=[Buf()])
                P.emit()

        def phase_D2(OT):
            with contextlib.ExitStack() as es:
                P = Prog(G)
                NB = L // 128
                tri = sbt(es, "d_tri", [128, 128], BF16)
                cb_ = Buf()

                def mk_tri(e):
                    e.memset(tri[:], 1.0)
                    return e.affine_select(out=tri[:], in_=tri[:], pattern=[[-1, 128]], compare_op=ALU.is_ge, fill=0.0, base=0, channel_multiplier=1)

                P.op("pool", mk_tri, [], [cb_])
                ktc = [sbt(es, "d_kt%d" % i, [128, L], BF16) for i in range(2)]
                qtc = [sbt(es, "d_qt%d" % i, [128, L], BF16) for i in range(2)]
                vc = [sbt(es, "d_v%d" % i, [128, NB, 128], BF16) for i in range(2)]
                lb = [Buf() for _ in range(2)]
                NE = 3
                et = [sbt(es, "d_e%d" % i, [128, 2, TS], F32) for i in range(NE)]
                eb = [Buf() for _ in range(NE)]
                lt = [sbt(es, "d_l%d" % i, [128, 2, TS], BF16) for i in range(NE)]
                ltb = [Buf() for _ in range(NE)]
                lacc = [sbt(es, "d_la%d" % i, [128, 2, TS], BF16) for i in range(2)]
                lab = [Buf() for _ in range(2)]
                gt = [sbt(es, "d_g%d" % i, [128, 2, TS], F32) for i in range(2)]
                gtb = [Buf() for _ in range(2)]
                wt = [sbt(es, "d_w%d" % i, [128, 2, TS], BF16) for i in range(3)]
                wtb = [Buf() for _ in range(3)]
                pz = [pst(es, "d_pz%d" % i, [128, 2, TS]) for i in range(2)]
                pzb = [Buf() for _ in range(2)]
                pr = pst(es, "d_pr", [128, 2, TS])
                prb = Buf()
                po = pst(es, "d_po", [128, 2, TS])
                pob = Buf()
                otb = Buf()
                steps = []
                for c in range(KC):
                    for qi in range(NT):
                        kbs = list(range(4 * qi + 3, -1, -1))
                        for n_, kb in enumerate(kbs):
                            steps.append((c, qi, kb, n_ == 0, n_ == len(kbs) - 1))
                NS = len(steps)
                loaded = set()

                def load_chunk(c):
                    if c in loaded or c >= KC:
                        return
                    loaded.add(c)
                    s = c % 2
                    P.dma("sp", ktc[s][:], KT.ap()[c], writes=[lb[s]])
                    P.dma("sp", qtc[s][:], QT.ap()[c], writes=[lb[s]])
                    for v4 in range(4):
                        P.dma("sp", vc[s][:, v4 * 8:(v4 + 1) * 8, :], VS.ap()[v4 * 8:(v4 + 1) * 8, :, c * 128:(c + 1) * 128].rearrange("b p d -> p b d"), writes=[lb[s]])

                def stage_A(t):
                    c, qi, kb, first, last = steps[t]
                    s = c % 2
                    z = t % 2
                    w = t % NE
                    ksl = slice(kb * 128, (kb + 1) * 128)
                    qsl = slice(qi * TS, (qi + 1) * TS)

                    def mmz(e, s=s, z=z, ksl=ksl, qsl=qsl):
                        e.matmul(pz[z][:, 0, :], lhsT=ktc[s][0:64, ksl], rhs=qtc[s][0:64, qsl], start=True, stop=True)
                        return e.matmul(pz[z][:, 1, :], lhsT=ktc[s][64:128, ksl], rhs=qtc[s][64:128, qsl], start=True, stop=True)

                    P.op("pe", mmz, [lb[s]], [pzb[z]])
                    P.op("act", lambda e, w=w, z=z: e.activation(out=et[w][:], in_=pz[z][:], func=AF.Exp), [pzb[z]], [eb[w]])
                    if kb >= 4 * qi:
                        base = qi * TS - kb * 128
                        P.op("pool", lambda e, w=w, base=base: e.affine_select(out=et[w][:], in_=et[w][:], pattern=[[0, 2], [1, TS]], compare_op=ALU.is_gt, fill=0.0, base=base, channel_multiplier=-1), [eb[w]], [eb[w]])

                def stage_A2(t):
                    w = t % NE
                    P.op("act", lambda e, w=w: e.activation(out=lt[w][:], in_=et[w][:], func=AF.Ln, bias=1.0, scale=1.0), [eb[w]], [ltb[w]])

                def stage_B1(t):
                    c, qi, kb, first, last = steps[t]
                    w = t % NE
                    la = t % 2
                    if first:
                        P.op("pool", lambda e, la=la: e.memset(lacc[la][:], 0.0), [], [lab[la]])

                    def mmr(e, w=w, la=la):
                        r = None
                        for hh in range(2):
                            e.matmul(pr[:, hh, :], lhsT=tri[:], rhs=lt[w][:, hh, :], start=True, stop=False)
                            r = e.matmul(pr[:, hh, :], lhsT=ones_bf[:], rhs=lacc[la][:, hh, :], start=False, stop=True)
                        return r

                    P.op("pe", mmr, [cb_, ltb[w], lab[la]], [prb])

                def stage_B2(t):
                    g2 = t % 2
                    P.op("act", lambda e, g2=g2: e.activation(out=gt[g2][:], in_=pr[:], func=AF.Exp, scale=-1.0), [prb], [gtb[g2]])

                def stage_B3(t):
                    c, qi, kb, first, last = steps[t]
                    w = t % NE
                    la = t % 2
                    g2 = t % 2
                    w3 = t % 3
                    P.op("dve", lambda e, w=w, g2=g2, w3=w3: e.tensor_tensor(out=wt[w3][:], in0=et[w][:], in1=gt[g2][:], op=ALU.mult), [eb[w], gtb[g2]], [wtb[w3]])
                    if not last:
                        P.op("dve", lambda e, w=w, la=la: e.tensor_tensor(out=lacc[1 - la][:], in0=lacc[la][:], in1=lt[w][:], op=ALU.add), [lab[la], ltb[w]], [lab[1 - la]])

                def stage_C(t):
                    c, qi, kb, first, last = steps[t]
                    s = c % 2
                    w3 = t % 3
                    qsl = slice(qi * TS, (qi + 1) * TS)

                    def mmo(e, s=s, w3=w3, kb=kb, first=first, last=last):
                        e.matmul(po[0:64, 0, :], lhsT=vc[s][:, kb, 0:64], rhs=wt[w3][:, 0, :], start=first, stop=last)
                        return e.matmul(po[64:128, 1, :], lhsT=vc[s][:, kb, 64:128], rhs=wt[w3][:, 1, :], start=first, stop=last)

                    P.op("pe", mmo, [lb[s], wtb[w3]], [pob])
                    if last:
                        P.op("act", lambda e, c=c, qsl=qsl: e.activation(out=OT[0:64, c, qsl], in_=po[0:64, 0, :], func=AF.Copy), [pob], [otb])
                        P.op("dve", lambda e, c=c, qsl=qsl: e.tensor_copy(out=OT[64:128, c, qsl], in_=po[64:128, 1, :]), [pob], [otb])

                load_chunk(0)
                for t in range(NS + 2):
                    if t < NS:
                        c = steps[t][0]
                        if steps[t][1] == 0 and steps[t][3]:
                            load_chunk(c)
                        if steps[t][1] == 2 and steps[t][3]:
                            load_chunk(c + 1)
                        stage_A(t)
                    if 0 <= t - 1 < NS:
                        stage_B1(t - 1)
                        stage_B2(t - 1)
                    if t < NS:
                        stage_A2(t)
                    if 0 <= t - 1 < NS:
                        stage_B3(t - 1)
                    if 0 <= t - 2 < NS:
                        stage_C(t - 2)
                P.emit()

        def phase_D3(OT):
            with contextlib.ExitStack() as es:
                P = Prog(G)
                w_sb = sbt(es, "o_w", [128, KC, D], BF16)
                wb = Buf()
                load_w_cast(P, lambda rc, c0, c1: w_sb[:, rc, c0:c1], w_o.ap(), KC, D, wb)
                xts = [sbt(es, "o_x%d" % i, [128, KC, TS], F32) for i in range(2)]
                xbs = [[Buf() for _ in range(KC)] for _ in range(2)]
                banks = [pst(es, "o_b%d" % i, [128, TS]) for i in range(4)]
                bbs = [Buf() for _ in range(4)]
                for i in range(NT):
                    xt, xb = xts[i % 2], xbs[i % 2]
                    tsl = slice(i * TS, (i + 1) * TS)
                    P.dma("sp", xt[:], xtile(XT, i), writes=xb)
                    for fo in range(KC):
                        bk, bb = banks[fo % 4], bbs[fo % 4]

                        def mm(e, fo=fo, bk=bk, tsl=tsl):
                            r = None
                            for kc in range(KC):
                                r = e.matmul(bk[:], lhsT=w_sb[:, kc, fo * 128:(fo + 1) * 128], rhs=OT[:, kc, tsl], start=(kc == 0), stop=(kc == KC - 1))
                            return r

                        P.op("pe", mm, [wb], [bb])
                        P.op("dve", lambda e, fo=fo, bk=bk, xt=xt: e.tensor_tensor(out=xt[:, fo, :], in0=bk[:], in1=xt[:, fo, :], op=ALU.add), [bb, xb[fo]], [xb[fo]])
                    P.dma("sp", xtile(XT, i), xt[:], reads=xb, writes=[Buf()])
                P.emit()

        init_consts()
        with contextlib.ExitStack() as esA:
            uT = sbt(esA, "uT", [128, KC, L], BF16)
            phase_A1(uT)
            if stop_after >= 1.5:
                phase_A2(uT)
            if stop_after >= 3:
                phase_A3(uT)
        if stop_after >= 4:
            phase_FFN(0, False)
        if stop_after >= 5:
            phase_CD()
        if stop_after >= 6:
            with contextlib.ExitStack() as esD:
                OT = sbt(esD, "OT", [128, KC, L], BF16)
                phase_D2(OT)
                if stop_after >= 7:
                    phase_D3(OT)
        if stop_after >= 8:
            phase_FFN(1, True)
    return nc


def prep_shared(inp):
    f = np.float32
    c = lambda a: np.ascontiguousarray(a, dtype=f)
    g6 = np.stack([inp["norm_mix"][0], inp["norm_ffn"][0], inp["norm_kv"], inp["norm_mix"][1], inp["norm_ffn"][1], inp["norm_final"]], 0)
    gam = c(g6.reshape(6, KC, 128).transpose(2, 0, 1))

    def pairs2(a):
        return c(a.reshape(32, 2, 64).transpose(1, 2, 0).reshape(128, 32))

    ldt = np.repeat(inp["ssm_log_dt"][0].reshape(32, 2, 1), 64, axis=2)
    d = {
        "gam": gam,
        "w_in": c(inp["ssm_w_in"][0]),
        "w_glu": c(inp["ssm_w_glu"][0]),
        "kv_w": c(inp["kv_w"]),
        "w_q": c(inp["attn_w_q"][0]),
        "w_o": c(inp["attn_w_o"][0]),
        "w_up": c(inp["ffn_w_up"]),
        "w_dn": c(inp["ffn_w_down"]),
        "cw": c(inp["ffn_conv_w"][:, :, 0, :].reshape(2, 3, FC, 128).transpose(3, 0, 2, 1)),
        "cb": c(inp["ffn_conv_b"].reshape(2, FC, 128).transpose(2, 0, 1)),
        "s_are": pairs2(inp["ssm_a_re"][0]),
        "s_aim": pairs2(inp["ssm_a_im"][0]),
        "s_ldt": c(ldt.transpose(1, 2, 0).reshape(128, 32)),
        "s_bre": c(inp["ssm_b_re"][0].reshape(32, 2, 64, 16).transpose(1, 2, 0, 3).reshape(128, 32, 16)),
        "s_bim": c(inp["ssm_b_im"][0].reshape(32, 2, 64, 16).transpose(1, 2, 0, 3).reshape(128, 32, 16)),
        "s_cre": c(inp["ssm_c_re"][0].reshape(32, 2, 16, 64).transpose(1, 3, 0, 2).reshape(128, 32, 16)),
        "s_cim": c(inp["ssm_c_im"][0].reshape(32, 2, 16, 64).transpose(1, 3, 0, 2).reshape(128, 32, 16)),
        "s_d": c(inp["ssm_d"][0].reshape(KC, 128).T),
    }
    return d


def kernel(**inputs):
    inp = {k: np.asarray(v) for k, v in inputs.items()}
    x = inp["x"]
    B = x.shape[0]
    shared = prep_shared(inp)
    in_maps = []
    for b in range(B):
        m = dict(shared)
        m["xin"] = np.ascontiguousarray(x[b].T).reshape(KC, 128, L)
        in_maps.append(m)
    nc = build()
    res = run_bass_kernel_spmd(nc, in_maps, core_ids=list(range(B)))
    out = np.empty((B, L, D), np.float32)
    for b in range(B):
        out[b] = res.results[b]["outT"].reshape(D, L).T
    return out
```
